# Optimizing a Trainium2 kernel written in Bass

```python
import numpy as np
import jax
import jax.numpy as jnp
from jax import lax

D_MODEL = 1024
BATCH = 16
SEQ = 4096
DEPTH = 2
DEC_BATCH = 8
DEC_SEQ = 8192
PAST_LEN = 128

GRID_W = 64
D_CONV = 256
CONV_K = 31
D_POOL = 256
POOL_WINDOWS = (2, 4, 8, 16)
N_POOL_GROUPS = len(POOL_WINDOWS)
POOL_GROUP = D_POOL // N_POOL_GROUPS
D_SC = 256
SC_K = 3
N_HEADS = 4
HEAD_DIM = 64
D_ATT = N_HEADS * HEAD_DIM
WIN_ROWS = 8
WIN_COLS = 16
Q_BLOCK_COLS = 16
KEY_BLOCK_COLS = 32
N_BRANCH = 4
D_FF = 2816
FFN_K = 3
EPS = 1e-6
NEG_INF = -1e30

OFF_A = 0
OFF_B = OFF_A + 2 * D_CONV
OFF_C = OFF_B + D_POOL
OFF_D = OFF_C + 3 * D_SC
OFF_G = OFF_D + 3 * D_ATT
D_IN = OFF_G + N_BRANCH * D_MODEL

kernel_name = 'hybrid_gated_parallel_encoder'


def rms_norm(x, g):
    xf = x.astype(jnp.float32)
    y = xf * lax.rsqrt(jnp.mean(xf * xf, axis=-1, keepdims=True) + EPS)
    return (y * g.astype(jnp.float32)).astype(x.dtype)


def layer_norm(x, g, b):
    xf = x.astype(jnp.float32)
    mu = jnp.mean(xf, axis=-1, keepdims=True)
    var = jnp.mean(jnp.square(xf - mu), axis=-1, keepdims=True)
    y = (xf - mu) * lax.rsqrt(var + EPS) * g.astype(jnp.float32) + b.astype(jnp.float32)
    return y.astype(x.dtype)


def depthwise_conv(x, w):
    k = w.shape[0]
    return lax.conv_general_dilated(
        x, w[:, None, :].astype(x.dtype), window_strides=(1,),
        padding=[(k // 2, k // 2)], dimension_numbers=('NWC', 'WIO', 'NWC'),
        feature_group_count=x.shape[-1])


def conformer_conv(za, conv_a_w, conv_a_b, ln_a_g, ln_a_b, w_out_a):
    a_val, a_gate = jnp.split(za, 2, axis=-1)
    a = a_val * jax.nn.sigmoid(a_gate)
    a = depthwise_conv(a, conv_a_w) + conv_a_b
    a = jax.nn.silu(layer_norm(a, ln_a_g, ln_a_b))
    return a @ w_out_a


def multiscale_pool(xb, pool_w, pool_scale, w_out_b):
    b, t, _ = xb.shape
    cs = jnp.concatenate([jnp.zeros((b, 1, D_POOL), jnp.float32),
                          lax.cumsum(xb.astype(jnp.float32), axis=1)], axis=1)
    pos = np.arange(t)
    outs = []
    for gi, w in enumerate(POOL_WINDOWS):
        lo = np.clip(pos - w // 2, 0, t - 1).astype(np.int32)
        hi = np.clip(pos + w // 2 - 1, 0, t - 1).astype(np.int32)
        cnt = (hi - lo + 1).astype(np.float32)
        sl = slice(gi * POOL_GROUP, (gi + 1) * POOL_GROUP)
        c = cs[..., sl]
        mean = (jnp.take(c, hi + 1, axis=1) - jnp.take(c, lo, axis=1)) / cnt[:, None]
        outs.append(mean - xb[..., sl].astype(jnp.float32))
    p = jnp.stack(outs, axis=2).astype(xb.dtype)
    p = jnp.einsum('btgc,gce->btge', p, pool_w).reshape(b, t, D_POOL) * pool_scale
    return p @ w_out_b


def short_gated_conv(zc, sc_w, w_out_c):
    gb, gc, xv = jnp.split(zc, 3, axis=-1)
    return (gb * depthwise_conv(gc * xv, sc_w)) @ w_out_c


def neighbourhood_attention(q, k, v, rpb):
    b, t = q.shape[:2]
    rows = t // GRID_W
    kr = min(WIN_ROWS, rows)
    r = np.arange(rows)
    rs = np.clip(r - kr // 2, 0, rows - kr)
    row_idx = (rs[:, None] + np.arange(kr)[None]).astype(np.int32)
    row_off = (row_idx - r[:, None] + (WIN_ROWS - 1)).astype(np.int32)
    n_cb = GRID_W // Q_BLOCK_COLS
    qcol = np.arange(GRID_W).reshape(n_cb, Q_BLOCK_COLS)
    cstart = np.clip(qcol - WIN_COLS // 2, 0, GRID_W - WIN_COLS)
    cb = np.clip(np.arange(n_cb) * Q_BLOCK_COLS - WIN_COLS // 2, 0, GRID_W - KEY_BLOCK_COLS)
    col_idx = (cb[:, None] + np.arange(KEY_BLOCK_COLS)[None]).astype(np.int32)
    kcol = col_idx[:, None, :]
    col_mask = (kcol >= cstart[..., None]) & (kcol < cstart[..., None] + WIN_COLS)
    col_off = np.clip(kcol - qcol[..., None] + WIN_COLS - 1, 0, 2 * WIN_COLS - 2).astype(np.int32)
    mask = jnp.asarray(col_mask)[None, None, :, :, None, :]
    col_bias = rpb[:, :, col_off].astype(jnp.float32)

    qg = q.reshape(b, rows, n_cb, Q_BLOCK_COLS, N_HEADS, HEAD_DIM).transpose(1, 0, 2, 3, 4, 5)
    kg = k.reshape(b, rows, GRID_W, N_HEADS, HEAD_DIM)
    vg = v.reshape(b, rows, GRID_W, N_HEADS, HEAD_DIM)

    def row_block(args):
        q_r, ridx, roff = args
        k_r = kg[:, ridx][:, :, col_idx]
        v_r = vg[:, ridx][:, :, col_idx]
        s = jnp.einsum('bjqhd,brjkhd->bhjqrk', q_r, k_r).astype(jnp.float32)
        s = s + col_bias[:, roff].transpose(0, 2, 3, 1, 4)[None]
        s = jnp.where(mask, s, NEG_INF)
        shp = s.shape
        p = jax.nn.softmax(s.reshape(shp[:4] + (-1,)), axis=-1).reshape(shp).astype(v.dtype)
        return jnp.einsum('bhjqrk,brjkhd->bjqhd', p, v_r)

    o = lax.map(row_block, (qg, jnp.asarray(row_idx), jnp.asarray(row_off)))
    return o.transpose(1, 0, 2, 3, 4, 5).reshape(b, t, D_ATT)


def token_mixer(h, w_in, b_gate, conv_a_w, conv_a_b, ln_a_g, ln_a_b, w_out_a, pool_w, pool_scale,
                w_out_b, sc_w, w_out_c, rpb, w_out_d, w_o):
    b, t, _ = h.shape
    z = h @ w_in
    br_a = conformer_conv(z[..., OFF_A:OFF_B], conv_a_w, conv_a_b, ln_a_g, ln_a_b, w_out_a)
    br_b = multiscale_pool(z[..., OFF_B:OFF_C], pool_w, pool_scale, w_out_b)
    br_c = short_gated_conv(z[..., OFF_C:OFF_D], sc_w, w_out_c)
    q, k, v = jnp.split(z[..., OFF_D:OFF_G], 3, axis=-1)
    q = (q * HEAD_DIM ** -0.5).reshape(b, t, N_HEADS, HEAD_DIM)
    k = k.reshape(b, t, N_HEADS, HEAD_DIM)
    v = v.reshape(b, t, N_HEADS, HEAD_DIM)
    br_d = neighbourhood_attention(q, k, v, rpb) @ w_out_d
    gates = jax.nn.sigmoid(z[..., OFF_G:].reshape(b, t, N_BRANCH, D_MODEL) + b_gate)
    merged = (gates[:, :, 0] * br_a + gates[:, :, 1] * br_b
              + gates[:, :, 2] * br_c + gates[:, :, 3] * br_d)
    return merged @ w_o


def conv_glu_ffn(h, w_up, ffn_conv_w, w_down):
    u = depthwise_conv(h @ w_up, ffn_conv_w)
    val, gate = jnp.split(u, 2, axis=-1)
    return (val * jax.nn.silu(gate)) @ w_down


def trunk(x, norm1_g, w_in, b_gate, conv_a_w, conv_a_b, ln_a_g, ln_a_b, w_out_a, pool_w, pool_scale,
          w_out_b, sc_w, w_out_c, rpb, w_out_d, w_o, norm2_g, w_up, ffn_conv_w, w_down, final_g):
    for l in range(DEPTH):
        h = rms_norm(x, norm1_g[l])
        x = x + token_mixer(h, w_in[l], b_gate[l], conv_a_w[l], conv_a_b[l], ln_a_g[l], ln_a_b[l],
                            w_out_a[l], pool_w[l], pool_scale[l], w_out_b[l], sc_w[l], w_out_c[l],
                            rpb[l], w_out_d[l], w_o[l])
        h = rms_norm(x, norm2_g[l])
        x = x + conv_glu_ffn(h, w_up[l], ffn_conv_w[l], w_down[l])
    return rms_norm(x, final_g)


def setup_inputs(seed: int = 0) -> dict:
    key = jax.random.key(seed)
    ks = jax.random.split(key, 24)
    f32 = jnp.float32

    def nrm(k, shape, s):
        return jax.random.normal(k, shape, f32) * s

    L = DEPTH
    return {
        'x_prompt': nrm(ks[0], (BATCH, SEQ, D_MODEL), 1.0),
        'x_sample': nrm(ks[1], (DEC_BATCH, DEC_SEQ, D_MODEL), 1.0),
        'norm1_g': 1.0 + nrm(ks[2], (L, D_MODEL), 0.05),
        'w_in': nrm(ks[3], (L, D_MODEL, D_IN), D_MODEL ** -0.5),
        'b_gate': nrm(ks[4], (L, N_BRANCH, D_MODEL), 0.1),
        'conv_a_w': nrm(ks[5], (L, CONV_K, D_CONV), CONV_K ** -0.5),
        'conv_a_b': nrm(ks[6], (L, D_CONV), 0.02),
        'ln_a_g': 1.0 + nrm(ks[7], (L, D_CONV), 0.05),
        'ln_a_b': nrm(ks[8], (L, D_CONV), 0.02),
        'w_out_a': nrm(ks[9], (L, D_CONV, D_MODEL), D_CONV ** -0.5),
        'pool_w': nrm(ks[10], (L, N_POOL_GROUPS, POOL_GROUP, POOL_GROUP), POOL_GROUP ** -0.5),
        'pool_scale': 1.0 + nrm(ks[11], (L, D_POOL), 0.1),
        'w_out_b': nrm(ks[12], (L, D_POOL, D_MODEL), D_POOL ** -0.5),
        'sc_w': nrm(ks[13], (L, SC_K, D_SC), SC_K ** -0.5),
        'w_out_c': nrm(ks[14], (L, D_SC, D_MODEL), D_SC ** -0.5),
        'rpb': nrm(ks[15], (L, N_HEADS, 2 * WIN_ROWS - 1, 2 * WIN_COLS - 1), 0.1),
        'w_out_d': nrm(ks[16], (L, D_ATT, D_MODEL), D_ATT ** -0.5),
        'w_o': nrm(ks[17], (L, D_MODEL, D_MODEL), D_MODEL ** -0.5),
        'norm2_g': 1.0 + nrm(ks[18], (L, D_MODEL), 0.05),
        'w_up': nrm(ks[19], (L, D_MODEL, 2 * D_FF), D_MODEL ** -0.5),
        'ffn_conv_w': nrm(ks[20], (L, FFN_K, 2 * D_FF), FFN_K ** -0.5),
        'w_down': nrm(ks[21], (L, D_FF, D_MODEL), D_FF ** -0.5),
        'final_g': 1.0 + nrm(ks[22], (D_MODEL,), 0.05),
    }


def reference(x_prompt, x_sample, norm1_g, w_in, b_gate, conv_a_w, conv_a_b, ln_a_g, ln_a_b, w_out_a,
              pool_w, pool_scale, w_out_b, sc_w, w_out_c, rpb, w_out_d, w_o, norm2_g, w_up, ffn_conv_w,
              w_down, final_g):
    y_prompt = trunk(x_prompt, norm1_g, w_in, b_gate, conv_a_w, conv_a_b, ln_a_g, ln_a_b, w_out_a,
                     pool_w, pool_scale, w_out_b, sc_w, w_out_c, rpb, w_out_d, w_o, norm2_g, w_up,
                     ffn_conv_w, w_down, final_g)
    y_sample = trunk(x_sample, norm1_g, w_in, b_gate, conv_a_w, conv_a_b, ln_a_g, ln_a_b, w_out_a,
                     pool_w, pool_scale, w_out_b, sc_w, w_out_c, rpb, w_out_d, w_o, norm2_g, w_up,
                     ffn_conv_w, w_down, final_g)
    return (y_prompt, y_sample)
```

```python
import numpy as np
from contextlib import ExitStack
import concourse.bass as bass
import concourse.mybir as mybir
from concourse.bass_utils import run_bass_kernel_spmd

F32 = mybir.dt.float32
BF16 = mybir.dt.bfloat16
AF = mybir.ActivationFunctionType
ALU = mybir.AluOpType

D = 1024
DIN = 6400
DFF = 2816
L = 2
TT = 512
GRID_W = 64
POOL_WINDOWS = (2, 4, 8, 16)
EPS = 1e-6
V_G1, V_G2, V_BG, V_CAB, V_LNG, V_LNB, V_PSC, V_CAW, V_SCW, V_FCW = 0, 8, 16, 48, 50, 52, 54, 56, 118, 124
V_FCT = 256
V_WST = 388
NV = 452


def MM(out, lhsT, rhs, start, stop):
    return lambda e: e.matmul(out, lhsT, rhs, start=start, stop=stop)


def TR(out, in_, ident):
    return lambda e: e.transpose(out, in_, ident)


def ACTF(out, in_, func, bias=None, scale=None, accum_out=None):
    kw = {}
    if bias is not None:
        kw["bias"] = bias
    if scale is not None:
        kw["scale"] = scale
    if accum_out is not None:
        kw["accum_out"] = accum_out
    return lambda e: e.activation(out=out, in_=in_, func=func, **kw)


def TTO(out, a, b, op):
    return lambda e: e.tensor_tensor(out=out, in0=a, in1=b, op=op)


def TS(out, a, s1, op0, s2=None, op1=None):
    if op1 is None:
        return lambda e: e.tensor_scalar(out=out, in0=a, scalar1=s1, scalar2=None, op0=op0)
    return lambda e: e.tensor_scalar(out=out, in0=a, scalar1=s1, scalar2=s2, op0=op0, op1=op1)


def STT(out, a, s, b, op0, op1):
    return lambda e: e.scalar_tensor_tensor(out=out, in0=a, scalar=s, in1=b, op0=op0, op1=op1)


def CP(out, a):
    return lambda e: e.tensor_copy(out=out, in_=a)


def MSET(out, v):
    return lambda e: e.memset(out, v)


def RECIP(out, a):
    return lambda e: e.reciprocal(out=out, in_=a)


class Res:
    __slots__ = ("lw", "rd")

    def __init__(self):
        self.lw = None
        self.rd = {}


class Tile:
    def __init__(self, t, nres=1):
        self.t = t
        self.rs = [Res() for _ in range(nres)]

    def __getitem__(self, k):
        return self.t[k]

    @property
    def r(self):
        return self.rs

    def ri(self, *idx):
        return [self.rs[i] for i in idx]


class Sched:
    def __init__(self, nc, es):
        self.nc = nc
        self.eng = {"pe": nc.tensor, "act": nc.scalar, "dve": nc.vector, "pool": nc.gpsimd, "sp": nc.sync}
        self.sem = {k: es.enter_context(nc.semaphore("s_" + k)) for k in self.eng}
        self.cnt = {k: 0 for k in self.eng}
        self.pools = {"ld": (0, 56), "st": (56, 16), "sw": (72, 8)}
        self.dsem = [es.enter_context(nc.semaphore(f"sd{i}")) for i in range(80)]
        self.dcnt = [0] * 80
        self.dnext = {"ld": 0, "st": 0, "sw": 0}
        self.waited = {k: {} for k in self.eng}
        self.nins = 0
        self.nops = 0

    def _semh(self, key):
        return self.sem[key] if isinstance(key, str) else self.dsem[key]

    def _wait(self, eng, key, val):
        w = self.waited[eng]
        if w.get(key, 0) >= val:
            return
        self.eng[eng].wait_ge(self._semh(key), val)
        w[key] = val
        self.nins += 1

    def _deps(self, eng, reads, writes):
        deps = {}
        for r in reads:
            if r.lw is not None:
                k, v = r.lw
                if v > deps.get(k, 0):
                    deps[k] = v
        for w in writes:
            if w.lw is not None:
                k, v = w.lw
                if v > deps.get(k, 0):
                    deps[k] = v
            for k, v in w.rd.items():
                if v > deps.get(k, 0):
                    deps[k] = v
        for k, v in deps.items():
            if k == "pe" and eng == "pe":
                continue
            self._wait(eng, k, v)

    def op(self, eng, fns, reads=(), writes=()):
        self.nops += 1
        self._deps(eng, reads, writes)
        e = self.eng[eng]
        ins = None
        if not isinstance(fns, (list, tuple)):
            fns = [fns]
        for f in fns:
            ins = f(e)
        self.nins += len(fns)
        self.cnt[eng] += 1
        ins.then_inc(self.sem[eng], 1)
        v = self.cnt[eng]
        for r in reads:
            r.rd[eng] = v
        for w in writes:
            w.lw = (eng, v)
            w.rd = {}

    def dma(self, q, out, in_, reads=(), writes=(), slow=False):
        self.nops += 1
        kind = "sw" if q == "pool" else ("st" if len(reads) else "ld")
        base, n = self.pools[kind]
        i = base + self.dnext[kind]
        self.dnext[kind] = (self.dnext[kind] + 1) % n
        if self.dcnt[i]:
            self._wait(q, i, self.dcnt[i])
        self._deps(q, reads, writes)
        if slow:
            self.eng[q].dma_start(out=out, in_=in_, allow_slow_non_contiguous=True).then_inc(self.dsem[i], 16)
        else:
            self.eng[q].dma_start(out=out, in_=in_).then_inc(self.dsem[i], 16)
        self.nins += 1
        self.dcnt[i] += 16
        v = self.dcnt[i]
        for r in reads:
            r.rd[i] = v
        for w in writes:
            w.lw = (i, v)
            w.rd = {}

    def barrier(self):
        for e in self.eng:
            for k in self.eng:
                if self.cnt[k]:
                    self._wait(e, k, self.cnt[k])
            for i, c in enumerate(self.dcnt):
                if c:
                    self._wait(e, i, c)


def rr(*tiles):
    out = []
    for t in tiles:
        out.extend(t.rs)
    return out


def build_nc(seq_lens, depth=L, debug=False, upto=None):
    Ttot = sum(seq_lens)
    seqs = []
    o = 0
    for n in seq_lens:
        seqs.append((o, n))
        o += n
    tiles = []
    for (s0, n) in seqs:
        nt = n // TT
        for j in range(nt):
            tiles.append((s0 + j * TT, s0, n, j, nt))

    nc = bass.Bass("TRN2", target_bir_lowering=False)

    def din(name, shape, dt=F32):
        return nc.dram_tensor(name, list(shape), dt, kind="ExternalInput").ap()

    def dscr(name, shape, dt):
        if debug:
            return nc.dram_tensor(name, list(shape), dt, kind="ExternalOutput").ap()
        return nc.dram_tensor(name, list(shape), dt).ap()

    x_d = din("x", [Ttot, D])
    w_in_d = din("w_in", [L, D, DIN])
    w_out_d = [din("w_out_" + n, [L, 256, D]) for n in "abcd"]
    w_o_d = din("w_o", [L, D, D])
    w_up_d = din("w_up", [L, D, 2 * DFF])
    w_dn_d = din("w_down", [L, DFF, D])
    vecs_d = din("vecs", [L, 128, NV])
    pbd_d = din("pbd", [L, 2, 128, 128])
    tbh_d = din("tbh", [L, 64, 4, 15, 64])
    cmf_d = din("cmf", [128, 14, 64])
    icnt_d = din("icnt", [4, 128, 2, TT])
    fgb_d = din("fgb", [128, D])
    y_d = nc.dram_tensor("y", [Ttot, D], F32, kind="ExternalOutput").ap()

    HT_d = dscr("HT", [D, Ttot], BF16)
    AT_d = dscr("AT", [256, Ttot], BF16)
    XB_d = dscr("XB", [256, Ttot], F32)
    GB_d = dscr("GB", [256, Ttot], F32)
    CX_d = dscr("CX", [256, Ttot], BF16)
    QT_d = dscr("QT", [256, Ttot], BF16)
    KT_d = dscr("KT", [256, Ttot], BF16)
    VT_d = dscr("VT", [Ttot, 512], BF16)
    FT_d = dscr("FT", [D, Ttot], BF16)
    XM_d = dscr("XM", [Ttot, D], F32)
    PA_d = dscr("PA", [Ttot, D], F32)
    XL_d = dscr("XL", [Ttot, D], F32)

    with ExitStack() as es:
        S = Sched(nc, es)

        uid = [0]

        def sb(stack, name, shape, dt, nres=1):
            uid[0] += 1
            return Tile(stack.enter_context(nc.sbuf_tensor(f"sb{uid[0]}_{name}", list(shape), dt)), nres)

        psum = [Tile(es.enter_context(nc.psum_tensor(f"ps{i}", [128, 512], F32))) for i in range(8)]
        pidx = [0]
        pring = [8]

        def nps():
            p = psum[pidx[0] % pring[0]]
            pidx[0] += 1
            return p

        identF = sb(es, "identF", [128, 128], F32)
        identB = sb(es, "identB", [128, 128], BF16)
        onesF = sb(es, "onesF", [128, 128], F32)
        epsT = sb(es, "epsT", [128, 1], F32)
        vecs = sb(es, "vecs", [128, L, NV], F32)

        S.op("pool", MSET(identF[:, :], 0.0), writes=identF.r)
        S.op("pool", lambda e: e.affine_select(out=identF[:, :], in_=identF[:, :], pattern=[[-1, 128]],
                                               compare_op=ALU.not_equal, fill=1.0, base=0, channel_multiplier=1),
             reads=identF.r, writes=identF.r)
        S.op("pool", CP(identB[:, :], identF[:, :]), reads=identF.r, writes=identB.r)
        S.op("pool", MSET(onesF[:, :], 1.0), writes=onesF.r)
        S.op("pool", MSET(epsT[:, :], EPS), writes=epsT.r)
        for l in range(L):
            S.dma("sp", vecs[:, l, :], vecs_d[l, :, :], writes=vecs.r)

        def vcol(l, c):
            return vecs[:, l, c:c + 1]

        def wload(dst_ap, src_ap, dst_tile):
            S.dma("pool", dst_ap, src_ap, writes=dst_tile.r)

        def norm1(xt_ap_of_b, x_res, hn, ssq, rstd, junk):
            S.op("dve", MSET(ssq[:, :], 0.0), writes=ssq.r)
            for b in range(4):
                S.op("act", ACTF(junk[:, b, :], xt_ap_of_b(b), AF.Square, accum_out=ssq[:, b:b + 1]),
                     reads=x_res(b), writes=junk.ri(b) + ssq.r)
            S.op("act", ACTF(rstd[:, :], ssq[:, :], AF.Sqrt, bias=epsT[:, 0:1], scale=1.0 / D),
                 reads=rr(ssq, epsT), writes=rstd.r)
            S.op("dve", RECIP(rstd[:, :], rstd[:, :]), reads=rstd.r, writes=rstd.r)
            for b in range(4):
                if b % 2 == 0:
                    S.op("dve", TS(hn[:, b, :], xt_ap_of_b(b), rstd[:, b:b + 1], ALU.mult),
                         reads=x_res(b) + rstd.r, writes=hn.ri(b))
                else:
                    S.op("act", lambda e, b=b: e.mul(out=hn[:, b, :], in_=xt_ap_of_b(b), mul=rstd[:, b:b + 1]),
                         reads=x_res(b) + rstd.r, writes=hn.ri(b))

        def norm2(hn, hT, l, gcol0):
            for c in range(8):
                pt = nps()
                ptb = pt[:, :].bitcast(BF16)
                S.op("pe", [TR(ptb[:, b * 128:(b + 1) * 128], hn[:, b, c * 128:(c + 1) * 128], identB[:, :])
                            for b in range(4)], reads=rr(hn, identB), writes=pt.r)
                if c % 2 == 0:
                    S.op("act", lambda e, c=c, ptb=ptb: e.mul(out=hT[:, c, :], in_=ptb[:, 0:512], mul=vcol(l, gcol0 + c)),
                         reads=rr(pt, vecs), writes=hT.ri(c))
                else:
                    S.op("dve", TS(hT[:, c, :], ptb[:, 0:512], vcol(l, gcol0 + c), ALU.mult),
                         reads=rr(pt, vecs), writes=hT.ri(c))

        def phase1(l, xin_d):
            with ExitStack() as st:
                wA = sb(st, "wA", [128, 8, 2304], BF16, 18)
                for jc in (2, 0, 3, 1, 4, 6, 5, 7, 8, 10, 9, 11, 12, 14, 13, 15, 16, 17):
                    S.dma("pool", wA[:, :, jc * 128:(jc + 1) * 128],
                          w_in_d[l, :, jc * 128:(jc + 1) * 128].rearrange("(c p) n -> p c n", p=128), writes=wA.ri(jc))
                xt = [sb(st, f"xt{i}", [128, 4, D], F32) for i in range(2)]
                hn = sb(st, "hn", [128, 4, D], BF16, 4)
                junk = sb(st, "junk", [128, 4, D], BF16, 4)
                ssq = sb(st, "ssq", [128, 4], F32)
                rstd = sb(st, "rstd", [128, 4], F32)
                hT = [sb(st, f"hT{i}", [128, 8, TT], BF16, 8) for i in range(2)]
                aT = [sb(st, f"aT{i}", [128, 2, TT], BF16, 2) for i in range(2)]
                xbT = [sb(st, f"xbT{i}", [128, 2, TT], F32, 2) for i in range(2)]
                gbT = [sb(st, f"gbT{i}", [128, 2, TT], F32, 2) for i in range(2)]
                cxT = [sb(st, f"cxT{i}", [128, 2, TT], BF16, 2) for i in range(2)]
                qT = [sb(st, f"qT{i}", [128, 2, TT], BF16, 2) for i in range(2)]
                kT = [sb(st, f"kT{i}", [128, 2, TT], BF16, 2) for i in range(2)]
                vT = [sb(st, f"vT{i}", [128, 4, 512], BF16, 4) for i in range(2)]
                for i in range(2):
                    S.op("dve", MSET(vT[i][:, :, :], 0.0), writes=vT[i].r)
                sg = [sb(st, f"sg{i}", [128, TT], F32) for i in range(4)]

                def load(ti):
                    t0 = tiles[ti][0]
                    S.dma("sp", xt[ti % 2][:, :, :], xin_d[t0:t0 + TT, :].rearrange("(b p) d -> p b d", p=128),
                          writes=xt[ti % 2].r)

                def do_norm1(ti):
                    X = xt[ti % 2]
                    norm1(lambda b: X[:, b, :], lambda b: X.r, hn, ssq, rstd, junk)

                load(0)
                do_norm1(0)
                for ti, (t0, s0, sl, j, nt) in enumerate(tiles):
                    pb = ti % 2
                    if ti + 1 < len(tiles):
                        load(ti + 1)
                    norm2(hn, hT[pb], l, V_G1)
                    H = hT[pb]
                    S.dma("sp", HT_d[:, t0:t0 + TT].rearrange("(c p) t -> p c t", p=128), H[:, :, :], reads=H.r)

                    def proj(jc):
                        pz = nps()
                        S.op("pe", [MM(pz[:, :], wA[:, kc, jc * 128:(jc + 1) * 128], H[:, kc, :], kc == 0, kc == 7)
                                    for kc in range(8)], reads=wA.ri(jc) + H.r, writes=pz.r)
                        return pz

                    for c in range(2):
                        pz = proj(2 + c)
                        S.op("act", ACTF(sg[c][:, :], pz[:, :], AF.Sigmoid), reads=pz.r, writes=sg[c].r)
                        pz = proj(c)
                        S.op("dve", TTO(aT[pb][:, c, :], pz[:, :], sg[c][:, :], ALU.mult), reads=rr(pz, sg[c]),
                             writes=aT[pb].ri(c))
                    for c in range(2):
                        pz = proj(4 + c)
                        S.op("act", ACTF(xbT[pb][:, c, :], pz[:, :], AF.Identity), reads=pz.r, writes=xbT[pb].ri(c))
                        pz = proj(6 + c)
                        S.op("act", ACTF(gbT[pb][:, c, :], pz[:, :], AF.Identity), reads=pz.r, writes=gbT[pb].ri(c))
                    if ti + 1 < len(tiles):
                        do_norm1(ti + 1)
                    for c in range(2):
                        pz = proj(8 + c)
                        S.op("act", ACTF(sg[2 + c][:, :], pz[:, :], AF.Identity), reads=pz.r, writes=sg[2 + c].r)
                        pz = proj(10 + c)
                        S.op("dve", TTO(cxT[pb][:, c, :], pz[:, :], sg[2 + c][:, :], ALU.mult),
                             reads=rr(pz, sg[2 + c]), writes=cxT[pb].ri(c))
                    for c in range(2):
                        pz = proj(12 + c)
                        S.op("act", ACTF(qT[pb][:, c, :], pz[:, :], AF.Identity, scale=0.125), reads=pz.r,
                             writes=qT[pb].ri(c))
                        pz = proj(14 + c)
                        S.op("dve", CP(kT[pb][:, c, :], pz[:, :]), reads=pz.r, writes=kT[pb].ri(c))
                    for b in range(4):
                        pv = nps()
                        S.op("pe", [MM(pv[:, 0:256], H[:, kc, b * 128:(b + 1) * 128], wA[:, kc, 2048:2304], kc == 0, kc == 7)
                                    for kc in range(8)], reads=wA.ri(16, 17) + H.r, writes=pv.r)
                        vdst = vT[pb][:, b, :].rearrange("p (h2 y d) -> p h2 y d", h2=2, y=4)[:, :, 0:4:3, :]
                        vsrc = pv[:, 0:256].rearrange("p (h2 hh d) -> p h2 hh d", h2=2, hh=2)
                        if b % 2:
                            S.op("dve", CP(vdst, vsrc), reads=pv.r, writes=vT[pb].ri(b))
                        else:
                            S.op("act", ACTF(vdst, vsrc, AF.Identity), reads=pv.r, writes=vT[pb].ri(b))
                    for (dd, tl) in ((AT_d, aT), (XB_d, xbT), (GB_d, gbT), (CX_d, cxT), (QT_d, qT), (KT_d, kT)):
                        S.dma("sp", dd[:, t0:t0 + TT].rearrange("(c p) t -> p c t", p=128), tl[pb][:, :, :],
                              reads=tl[pb].r)
                    S.dma("sp", VT_d[t0:t0 + TT, :].rearrange("(b p) f -> p b f", p=128), vT[pb][:, :, :],
                          reads=vT[pb].r)
                S.barrier()

        def phase2a(l):
            with ExitStack() as st:
                L4 = sb(st, "L4", [128, 8, 8, 32], BF16, 64)
                identS = sb(st, "identS", [128, 32], F32)
                for s_ in range(4):
                    S.op("dve", CP(identS[32 * s_:32 * s_ + 32, :], identF[32 * s_:32 * s_ + 32, 32 * s_:32 * s_ + 32]),
                         reads=identF.r, writes=identS.r)
                for g in range(8):
                    for q in range(8):
                        S.op("dve", TS(L4[:, g, q, :], identS[:, :], vcol(l, V_WST + g * 8 + q), ALU.mult),
                             reads=rr(identS, vecs), writes=L4.ri(g * 8 + q))
                diagC = sb(st, "diagC", [128, 2, 3, 128], BF16, 6)
                pbd = sb(st, "pbd", [128, 2, 128], BF16)
                Tb = sb(st, "Tb", [128, 4, 14, 64], F32, 4)
                onesP = sb(st, "onesP", [128, 2, 128], BF16)
                for c in range(2):
                    for k in range(3):
                        S.op("dve", TS(diagC[:, c, k, :], identF[:, :], vcol(l, V_SCW + c * 3 + k), ALU.mult),
                             reads=rr(identF, vecs), writes=diagC.ri(c * 3 + k))
                    wload(pbd[:, c, :], pbd_d[l, c, :, :], pbd)
                S.op("pool", MSET(onesP[:, :, :], 0.0), writes=onesP.r)
                S.op("pool", MSET(onesP[:, 0, 0:64], 1.0), writes=onesP.r)
                S.op("pool", MSET(onesP[:, 1, 64:128], 1.0), writes=onesP.r)
                with ExitStack() as st2:
                    cmf = sb(st2, "cmf", [128, 14, 64], F32)
                    S.dma("sp", cmf[:, :, :], cmf_d[:, :, :], writes=cmf.r)
                    S.dma("sp", Tb[0:64, :, :, :], tbh_d[l, :, :, 0:14, :], writes=Tb.r)
                    S.dma("sp", Tb[64:128, :, :, :], tbh_d[l, :, :, 1:15, :], writes=Tb.r)
                    for h in range(4):
                        S.op("dve", TTO(Tb[:, h, :, :], Tb[:, h, :, :], cmf[:, :, :], ALU.add), reads=Tb.ri(h) + cmf.r,
                             writes=Tb.ri(h))
                    S.barrier()

                xbH = [sb(st, f"xbH{i}", [128, 2, TT + 16], F32, 3) for i in range(2)]
                cxH = [sb(st, f"cxH{i}", [128, 2, TT + 2], BF16, 3) for i in range(2)]
                gbH = [sb(st, f"gbH{i}", [128, 2, TT], F32) for i in range(2)]
                aS = [sb(st, f"aS{i}", [128, 8, TT + 28], BF16, 32) for i in range(2)]
                qH = [sb(st, f"qH{i}", [128, 2, TT], BF16) for i in range(2)]
                kH = [sb(st, f"kH{i}", [128, 2, 15 * 64], BF16) for i in range(2)]
                icn = [sb(st, f"icn{i}", [128, 2, TT], F32) for i in range(2)]
                Vpd = [sb(st, f"Vpad{i}", [128, 14, 4, 128], BF16, 14) for i in range(2)]
                PT = [sb(st, f"PT{i}", [128, TT], BF16) for i in range(16)]
                sbs = [sb(st, f"sbs{i}", [128, TT], F32) for i in range(3)]
                cb = sb(st, "cb", [128, 2, TT], F32, 2)
                sq = sb(st, "sq", [128, 2, TT], F32, 2)
                mu = sb(st, "mu", [128, TT], F32)
                msq = sb(st, "msq", [128, TT], F32)
                rsd = sb(st, "rsd", [128, TT], F32)
                yln = sb(st, "yln", [128, 2, TT], F32, 2)
                s2 = sb(st, "s2", [128, 2, TT + 16], F32, 2)
                s4 = sb(st, "s4", [128, 2, TT + 16], F32, 2)
                s8 = sb(st, "s8", [128, TT + 16], F32)
                s16 = sb(st, "s16", [128, TT + 16], F32)
                pmean = sb(st, "pmean", [128, 2, TT], F32, 4)
                ppd = [sb(st, f"pp{i}", [128, 2, TT], BF16, 4) for i in range(2)]
                rec = [sb(st, f"rec{i}", [128, TT], F32) for i in range(2)]
                Fo = [sb(st, f"Fo{i}", [128, 8, TT], BF16, 8) for i in range(2)]
                for i in range(2):
                    S.op("pool", MSET(Vpd[i][:, :, :, :], 0.0), writes=Vpd[i].r)
                for i in range(2):
                    S.op("dve", MSET(aS[i][:, :, :], 0.0), writes=aS[i].r)
                pring[0] = 6

                def rowinfo(ti):
                    t0, s0, sl, j, nt = tiles[ti]
                    rows = sl // GRID_W
                    r0 = j * 8
                    rs = [min(max(r0 + q - 4, 0), rows - 8) for q in range(8)]
                    lo = rs[0]
                    hi = rs[7] + 8
                    return rows, r0, rs, lo, hi

                def halo_load(dst, dd, ti, hl, hr):
                    t0, s0, sl, j, nt = tiles[ti]
                    S.dma("sp", dst[:, :, hl:hl + TT], dd[:, t0:t0 + TT].rearrange("(c p) t -> p c t", p=128),
                          writes=dst.ri(0))
                    if hl:
                        if j == 0:
                            S.op("pool", MSET(dst[:, :, 0:hl], 0.0), writes=dst.ri(1))
                        else:
                            S.dma("sp", dst[:, :, 0:hl], dd[:, t0 - hl:t0].rearrange("(c p) t -> p c t", p=128),
                                  writes=dst.ri(1), slow=(hl == 1))
                    if hr:
                        if j == nt - 1:
                            S.op("pool", MSET(dst[:, :, hl + TT:hl + TT + hr], 0.0), writes=dst.ri(2))
                        else:
                            S.dma("sp", dst[:, :, hl + TT:hl + TT + hr],
                                  dd[:, t0 + TT:t0 + TT + hr].rearrange("(c p) t -> p c t", p=128), writes=dst.ri(2),
                                  slow=(hr == 1))

                def load(ti):
                    t0, s0, sl, j, nt = tiles[ti]
                    pb = ti % 2
                    if j == 0:
                        S.op("pool", MSET(aS[pb][:, :, 0:15], 0.0), writes=aS[pb].r)
                    if j == nt - 1:
                        S.op("pool", MSET(aS[pb][:, :, TT + 12:TT + 28], 0.0), writes=aS[pb].r)
                    for g in range(8):
                        for s_ in range(4):
                            c_lo = (15 - s_) if j == 0 else 0
                            c_hi = (TT + 15 - s_) if j == nt - 1 else TT + 28
                            tk = t0 - 15 + s_
                            S.dma("sp", aS[pb][32 * s_:32 * s_ + 32, g, c_lo:c_hi],
                                  AT_d[g * 32:(g + 1) * 32, tk + c_lo:tk + c_hi], writes=aS[pb].ri(g * 4 + s_))
                    halo_load(cxH[pb], CX_d, ti, 1, 1)
                    halo_load(qH[pb], QT_d, ti, 0, 0)
                    rows, r0, rs, lo, hi = rowinfo(ti)
                    k0 = s0 + lo * 64
                    nk = (hi - lo) * 64
                    S.dma("sp", kH[pb][:, :, 0:nk], KT_d[:, k0:k0 + nk].rearrange("(c p) t -> p c t", p=128),
                          writes=kH[pb].r)
                    ns = hi - lo - 1
                    for s in range(ns):
                        ks = k0 + s * 64
                        S.dma("sp", Vpd[pb][:, s, :, :].rearrange("p h c -> p (h c)"), VT_d[ks:ks + 128, :],
                              writes=Vpd[pb].ri(s))
                    halo_load(xbH[pb], XB_d, ti, 8, 8)
                    halo_load(gbH[pb], GB_d, ti, 0, 0)
                    var = (1 if j == 0 else 0) + (2 if j == nt - 1 else 0)
                    S.dma("sp", icn[pb][:, :, :], icnt_d[var, :, :, :], writes=icn[pb].r)

                def p2a_tail(F, pp_, t0):
                    pm = nps()
                    S.op("pe", [MM(pm[:, :], onesF[:, :], cb[:, c, :], c == 0, c == 1) for c in range(2)],
                         reads=rr(onesF, cb), writes=pm.r)
                    pq = nps()
                    S.op("pe", [MM(pq[:, :], onesF[:, :], sq[:, c, :], c == 0, c == 1) for c in range(2)],
                         reads=rr(onesF, sq), writes=pq.r)
                    S.op("dve", TS(mu[:, :], pm[:, :], 1.0 / 256, ALU.mult), reads=pm.r, writes=mu.r)
                    S.op("dve", TTO(msq[:, :], mu[:, :], mu[:, :], ALU.mult), reads=mu.r, writes=msq.r)
                    S.op("dve", STT(rsd[:, :], pq[:, :], 1.0 / 256, msq[:, :], ALU.mult, ALU.subtract),
                         reads=rr(pq, msq), writes=rsd.r)
                    S.op("act", ACTF(rsd[:, :], rsd[:, :], AF.Ln, bias=epsT[:, 0:1]), reads=rr(rsd, epsT),
                         writes=rsd.r)
                    S.op("act", ACTF(rsd[:, :], rsd[:, :], AF.Exp, scale=-0.5), reads=rsd.r, writes=rsd.r)
                    for c in range(2):
                        S.op("dve", TTO(yln[:, c, :], cb[:, c, :], mu[:, :], ALU.subtract), reads=cb.ri(c) + mu.r,
                             writes=yln.ri(c))
                        S.op("dve", TTO(yln[:, c, :], yln[:, c, :], rsd[:, :], ALU.mult), reads=yln.ri(c) + rsd.r,
                             writes=yln.ri(c))
                        S.op("act", ACTF(F[:, c, :], yln[:, c, :], AF.Silu, bias=vcol(l, V_LNB + c),
                                         scale=vcol(l, V_LNG + c)), reads=yln.ri(c) + vecs.r, writes=F.ri(c))
                    for c in range(2):
                        pc = nps()
                        S.op("pe", [MM(pc[:, :], pbd[:, c, :], pp_[:, c, :], True, True)],
                             reads=pbd.r + pp_.ri(2 * c, 2 * c + 1), writes=pc.r)
                        S.op("act", lambda e, c=c, pc=pc: e.mul(out=F[:, 2 + c, :], in_=pc[:, :], mul=vcol(l, V_PSC + c)),
                             reads=rr(pc, vecs), writes=F.ri(2 + c))
                    S.dma("sp", FT_d[:, t0:t0 + TT].rearrange("(c p) t -> p c t", p=128), F[:, :, :], reads=F.r)

                load(0)
                for ti, (t0, s0, sl, j, nt) in enumerate(tiles):
                    pb = ti % 2
                    if ti + 1 < len(tiles):
                        load(ti + 1)
                    rows, r0, rs, lo, hi = rowinfo(ti)
                    ns = hi - lo - 1
                    pp = ppd[pb]
                    XBH, CXH, GBH, QH, KH, Vpad, ICN, F = xbH[pb], cxH[pb], gbH[pb], qH[pb], kH[pb], Vpd[pb], icn[pb], Fo[pb]
                    pcA = [psum[6], psum[7]]
                    conv_sub = [(c, q) for c in range(2) for q in range(8)]
                    AS = aS[pb]
                    sbi = 0
                    PTs = {}
                    n_it = 0
                    for cpair in range(2):
                        for i in range(4):
                            for hh in range(2):
                                h = cpair * 2 + hh
                                hb = hh * 64
                                pS = nps()
                                mms = []
                                for q in range(8):
                                    kc0 = (rs[q] + 2 * i - lo) * 64
                                    mms.append(MM(pS[:, q * 64:(q + 1) * 64], KH[hb:hb + 64, cpair, kc0:kc0 + 128],
                                                  QH[hb:hb + 64, cpair, q * 64:(q + 1) * 64], True, True))
                                S.op("pe", mms, reads=rr(KH, QH), writes=pS.r)
                                sbt = sbs[sbi % 3]
                                sbi += 1
                                mis = [rs[q] + 2 * i - (r0 + q) + 7 for q in range(8)]
                                q = 0
                                while q < 8:
                                    q2 = q
                                    while q2 + 1 < 8 and mis[q2 + 1] == mis[q]:
                                        q2 += 1
                                    n = q2 - q + 1
                                    S.op("dve", TTO(sbt[:, q * 64:(q2 + 1) * 64].rearrange("p (a b) -> p a b", b=64),
                                                    pS[:, q * 64:(q2 + 1) * 64].rearrange("p (a b) -> p a b", b=64),
                                                    Tb[:, h, mis[q]:mis[q] + 1, :].to_broadcast([128, n, 64]), ALU.add),
                                         reads=rr(pS, Tb), writes=sbt.r)
                                    q = q2 + 1
                                ptile = PT[n_it]
                                S.op("act", ACTF(ptile[:, :], sbt[:, :], AF.Exp), reads=sbt.r, writes=ptile.r)
                                PTs[(cpair, i, hh)] = ptile
                                c, q = conv_sub[n_it]
                                S.op("pe", [(lambda e, c=c, q=q, jj=jj: e.matmul(
                                    pcA[c][32 * jj:32 * jj + 32, :], L4[:, 4 * c + jj, q, :],
                                    AS[:, 4 * c + jj, 4 * q:4 * q + TT], start=(q == 0), stop=(q == 7),
                                    tile_position=(0, 32 * jj))) for jj in range(4)],
                                     reads=rr(L4, AS), writes=pcA[c].r)
                                n_it += 1
                    W = TT + 16
                    for c in range(2):
                        S.op("dve", TTO(s2[:, c, 1:W - 1], XBH[:, c, 0:W - 2], XBH[:, c, 1:W - 1], ALU.add),
                             reads=XBH.r, writes=s2.ri(c))
                        S.op("dve", TTO(s4[:, c, 2:W - 2], s2[:, c, 1:W - 3], s2[:, c, 3:W - 1], ALU.add),
                             reads=s2.ri(c), writes=s4.ri(c))
                    S.op("dve", TTO(s8[:, 4:W - 4], s4[:, 1, 2:W - 6], s4[:, 1, 6:W - 2], ALU.add), reads=s4.ri(1),
                         writes=s8.r)
                    S.op("dve", TTO(s16[:, 8:W - 8], s8[:, 4:W - 12], s8[:, 12:W - 4], ALU.add), reads=s8.r,
                         writes=s16.r)
                    srcs = [(s2, 0, 0, 64), (s4, 0, 64, 128), (s8, None, 0, 64), (s16, None, 64, 128)]
                    for g, (stl, cidx, p0, p1) in enumerate(srcs):
                        c = g // 2
                        src = stl[p0:p1, cidx, 8:8 + TT] if cidx is not None else stl[p0:p1, 8:8 + TT]
                        S.op("dve", TTO(pmean[p0:p1, c, :], src, ICN[p0:p1, c, :], ALU.mult), reads=rr(stl, ICN),
                             writes=pmean.ri(g))
                        S.op("dve", TTO(pp[p0:p1, c, :], pmean[p0:p1, c, :], XBH[p0:p1, c, 8:8 + TT], ALU.subtract),
                             reads=pmean.ri(g) + XBH.r, writes=pp.ri(g))
                    for c in range(2):
                        pc = nps()
                        S.op("pe", [MM(pc[:, :], diagC[:, c, k, :], CXH[:, c, k:k + TT], k == 0, k == 2)
                                    for k in range(3)], reads=rr(diagC, CXH), writes=pc.r)
                        S.op("dve", TTO(F[:, 4 + c, :], pc[:, :], GBH[:, c, :], ALU.mult), reads=rr(pc, GBH),
                             writes=F.ri(4 + c))
                    if ti > 0:
                        p2a_tail(Fo[(ti - 1) % 2], ppd[(ti - 1) % 2], tiles[ti - 1][0])
                    for c in range(2):
                        S.op("act", ACTF(cb[:, c, :], pcA[c][:, :], AF.Identity, bias=vcol(l, V_CAB + c)),
                             reads=rr(pcA[c], vecs), writes=cb.ri(c))
                        S.op("act", ACTF(sq[:, c, :], cb[:, c, :], AF.Square), reads=cb.ri(c), writes=sq.ri(c))
                    for cpair in range(2):
                        pts = [PTs[(cpair, i, hh)] for hh in range(2) for i in range(4)]
                        po = nps()
                        mms = []
                        for q in range(8):
                            for i in range(4):
                                s = rs[q] + 2 * i - lo
                                for hh in range(2):
                                    h = cpair * 2 + hh
                                    o = hh * 64
                                    mms.append(lambda e, q=q, i=i, s=s, h=h, hh=hh, o=o, po=po: e.matmul(
                                        po[o:o + 64, q * 64:(q + 1) * 64], Vpad[:, s, h, o:o + 64],
                                        PTs[(cpair, i, hh)][:, q * 64:(q + 1) * 64], start=(i == 0), stop=(i == 3),
                                        tile_position=(0, o)))
                        S.op("pe", mms, reads=rr(Vpad, *pts), writes=po.r)
                        pd = nps()
                        mms = []
                        for i in range(4):
                            for hh in range(2):
                                o = hh * 64
                                mms.append(lambda e, i=i, hh=hh, o=o, pd=pd: e.matmul(
                                    pd[o:o + 64, :], onesP[:, hh, o:o + 64], PTs[(cpair, i, hh)][:, :],
                                    start=(i == 0), stop=(i == 3), tile_position=(0, o)))
                        S.op("pe", mms, reads=rr(onesP, *pts), writes=pd.r)
                        S.op("act", ACTF(rec[cpair][:, :], pd[:, :], AF.Ln), reads=pd.r, writes=rec[cpair].r)
                        S.op("act", ACTF(rec[cpair][:, :], rec[cpair][:, :], AF.Exp, scale=-1.0), reads=rec[cpair].r,
                             writes=rec[cpair].r)
                        S.op("dve", TTO(F[:, 6 + cpair, :], po[:, :], rec[cpair][:, :], ALU.mult),
                             reads=rr(po, rec[cpair]), writes=F.ri(6 + cpair))
                if tiles:
                    tl = tiles[-1]
                    p2a_tail(Fo[(len(tiles) - 1) % 2], ppd[(len(tiles) - 1) % 2], tl[0])
                S.barrier()
                pring[0] = 8

        def phase2b(l, xin_d):
            with ExitStack() as st:
                wG = sb(st, "wG", [128, 8, 4096], BF16, 32)
                wOut = sb(st, "wOut", [128, 4, 2, D], BF16, 8)
                wO = sb(st, "wO", [128, 8, D], BF16, 8)
                for i in range(4):
                    for kc in range(2):
                        S.dma("pool", wOut[:, i, kc, :], w_out_d[i][l, kc * 128:(kc + 1) * 128, :],
                              writes=wOut.ri(i * 2 + kc))
                for c in range(8):
                    for i in range(4):
                        c0 = i * D + c * 128
                        S.dma("pool", wG[:, :, c0:c0 + 128],
                              w_in_d[l, :, 2304 + c0:2304 + c0 + 128].rearrange("(c p) n -> p c n", p=128),
                              writes=wG.ri(i * 8 + c))
                for kc in range(8):
                    S.dma("pool", wO[:, kc, :], w_o_d[l, kc * 128:(kc + 1) * 128, :], writes=wO.ri(kc))
                hT = [sb(st, f"hTb{i}", [128, 8, TT], BF16) for i in range(2)]
                Fi = [sb(st, f"Fi{i}", [128, 8, TT], BF16) for i in range(2)]
                xblk = [sb(st, f"xblk{i}", [128, D], F32) for i in range(2)]
                oblk = [sb(st, f"oblk{i}", [128, D], F32, 2) for i in range(2)]
                sgr = [sb(st, f"sgr{i}", [128, TT], F32) for i in range(3)]
                tmr = [sb(st, f"tmr{i}", [128, TT], F32) for i in range(3)]
                Mc = [sb(st, f"Mc{i}", [128, TT], F32) for i in range(2)]
                Mb = sb(st, "Mb", [128, 8, TT], BF16, 8)

                def load(ti):
                    t0 = tiles[ti][0]
                    pb = ti % 2
                    S.dma("sp", hT[pb][:, :, :], HT_d[:, t0:t0 + TT].rearrange("(c p) t -> p c t", p=128),
                          writes=hT[pb].r)
                    S.dma("sp", Fi[pb][:, :, :], FT_d[:, t0:t0 + TT].rearrange("(c p) t -> p c t", p=128),
                          writes=Fi[pb].r)

                load(0)
                nsg = 0
                nb = 0
                for ti, (t0, s0, sl, j, nt) in enumerate(tiles):
                    pb = ti % 2
                    if ti + 1 < len(tiles):
                        load(ti + 1)
                    H, F = hT[pb], Fi[pb]
                    for c in range(8):
                        mc = Mc[c % 2]
                        for i in range(4):
                            pg = nps()
                            S.op("pe", [MM(pg[:, :], wG[:, kc, i * D + c * 128:i * D + (c + 1) * 128], H[:, kc, :],
                                           kc == 0, kc == 7) for kc in range(8)], reads=wG.ri(i * 8 + c) + H.r,
                                 writes=pg.r)
                            pbr = nps()
                            S.op("pe", [MM(pbr[:, :], wOut[:, i, kc, c * 128:(c + 1) * 128], F[:, i * 2 + kc, :],
                                           kc == 0, kc == 1) for kc in range(2)],
                                 reads=wOut.ri(i * 2, i * 2 + 1) + F.r, writes=pbr.r)
                            sgt = sgr[nsg % 3]
                            S.op("act", ACTF(sgt[:, :], pg[:, :], AF.Sigmoid, bias=vcol(l, V_BG + i * 8 + c)),
                                 reads=rr(pg, vecs), writes=sgt.r)
                            if i == 0:
                                S.op("dve", TTO(mc[:, :], pbr[:, :], sgt[:, :], ALU.mult), reads=rr(pbr, sgt),
                                     writes=mc.r)
                            else:
                                tm = tmr[nsg % 3]
                                S.op("dve", TTO(tm[:, :], pbr[:, :], sgt[:, :], ALU.mult), reads=rr(pbr, sgt),
                                     writes=tm.r)
                                if i < 3:
                                    S.op("dve", TTO(mc[:, :], mc[:, :], tm[:, :], ALU.add), reads=rr(mc, tm),
                                         writes=mc.r)
                                else:
                                    S.op("dve", TTO(Mb[:, c, :], mc[:, :], tm[:, :], ALU.add), reads=rr(mc, tm),
                                         writes=Mb.ri(c))
                            nsg += 1
                    for b in range(4):
                        xb_ = xblk[nb % 2]
                        ob_ = oblk[nb % 2]
                        nb += 1
                        S.dma("sp", xb_[:, :], xin_d[t0 + b * 128:t0 + (b + 1) * 128, :], writes=xb_.r)
                        for f in range(2):
                            po = nps()
                            S.op("pe", [MM(po[:, :], Mb[:, kc, b * 128:(b + 1) * 128], wO[:, kc, f * 512:(f + 1) * 512],
                                           kc == 0, kc == 7) for kc in range(8)], reads=rr(Mb, wO), writes=po.r)
                            S.op("dve", TTO(ob_[:, f * 512:(f + 1) * 512], po[:, :], xb_[:, f * 512:(f + 1) * 512], ALU.add),
                                 reads=rr(po, xb_), writes=ob_.ri(f))
                        S.dma("sp", XM_d[t0 + b * 128:t0 + (b + 1) * 128, :], ob_[:, :], reads=ob_.r)
                S.barrier()

        def phase3(l, half, res_d, dst_d, final):
            v0 = half * 11
            with ExitStack() as st:
                wUp = sb(st, "wUp", [128, 8, 2, 11 * 128], BF16, 22)
                wDn = sb(st, "wDn", [128, 11, D], BF16, 11)
                for v in range(11):
                    for vg in (1, 0):
                        c0 = vg * DFF + (v0 + v) * 128
                        S.dma("pool", wUp[:, :, vg, v * 128:(v + 1) * 128],
                              w_up_d[l, :, c0:c0 + 128].rearrange("(c p) n -> p c n", p=128), writes=wUp.ri(vg * 11 + v))
                for v in range(11):
                    S.dma("pool", wDn[:, v, :], w_dn_d[l, (v0 + v) * 128:(v0 + v + 1) * 128, :], writes=wDn.ri(v))
                fg = sb(st, "fg", [128, D], F32)
                if final:
                    S.dma("sp", fg[:, :], fgb_d[:, :], writes=fg.r)
                xt = [sb(st, f"x3{i}", [128, 4, D], F32) for i in range(2)]
                hn = sb(st, "hn3", [128, 4, D], BF16, 4)
                junk = sb(st, "junk3", [128, 4, D], BF16, 4)
                ss4 = sb(st, "ss4", [128, 4], F32)
                rs4 = sb(st, "rs4", [128, 4], F32)
                ssq = sb(st, "ssq3", [128, 4], F32)
                rstd = sb(st, "rstd3", [128, 4], F32)
                hTs = [sb(st, f"hT3{i}", [128, 8, TT], BF16, 8) for i in range(2)]
                G = sb(st, "G3", [128, 11, TT], BF16, 11)
                ur = [sb(st, f"u3{i}", [128, TT], F32) for i in range(4)]
                sgt = [sb(st, f"sg3{i}", [128, TT], F32) for i in range(2)]
                ycar = [sb(st, f"ycar{i}", [128, 22, 2], F32, 22) for i in range(2)]
                ss2 = sb(st, "ss2", [128, 2], F32)
                rs2 = sb(st, "rs2", [128, 2], F32)
                ucol = sb(st, "ucol", [128, 22], F32)
                ucol2 = sb(st, "ucol2", [128, 22], F32)
                gcol = sb(st, "gcol", [128, 11], BF16)

                def fcw(cc_global, k):
                    return vcol(l, V_FCW + cc_global * 3 + k)

                def load(ti):
                    t0 = tiles[ti][0]
                    S.dma("sp", xt[ti % 2][:, :, :], XM_d[t0:t0 + TT, :].rearrange("(b p) d -> p b d", p=128),
                          writes=xt[ti % 2].r)

                nu = [0]
                nx = [0]
                wk = sb(st, "wk", [128, 3, 22], F32)
                for k in range(3):
                    for vg in range(2):
                        c0 = V_FCT + k * 44 + vg * 22 + v0
                        S.op("dve", CP(wk[:, k, vg * 11:(vg + 1) * 11], vecs[:, l, c0:c0 + 11]), reads=vecs.r,
                             writes=wk.r)
                cf = sb(st, "cf", [128, 22, 2], F32)
                cft = sb(st, "cft", [128, 22], F32)
                cft2 = sb(st, "cft2", [128, 22], F32)
                xl = [sb(st, f"xl3{i}", [128, D], F32, 2) for i in range(6)]

                def out_finish(xl_, np_, tok_lo, p_lo):
                    if final:
                        k = nx[0] % 2
                        S.op("dve", MSET(ss2[:, k:k + 1], 0.0), writes=ss2.r)
                        S.op("act", ACTF(junk[0:np_, 0, :], xl_[0:np_, :], AF.Square, accum_out=ss2[0:np_, k:k + 1]),
                             reads=xl_.r, writes=rr(junk, ss2))
                        S.op("act", ACTF(rs2[0:np_, k:k + 1], ss2[0:np_, k:k + 1], AF.Sqrt, bias=epsT[0:np_, 0:1],
                                         scale=1.0 / D), reads=rr(ss2, epsT), writes=rs2.r)
                        S.op("dve", RECIP(rs2[0:np_, k:k + 1], rs2[0:np_, k:k + 1]), reads=rs2.r, writes=rs2.r)
                        S.op("dve", STT(xl_[0:np_, :], xl_[0:np_, :], rs2[0:np_, k:k + 1], fg[0:np_, :],
                                        ALU.mult, ALU.mult), reads=rr(xl_, rs2, fg), writes=xl_.r)
                    S.dma("sp", dst_d[tok_lo + p_lo:tok_lo + np_, :], xl_[p_lo:np_, :], reads=xl_.r)

                def res_load(np_, tok_lo, p_lo):
                    xl_ = xl[nx[0] % 6]
                    nx[0] += 1
                    if p_lo:
                        S.op("dve", MSET(xl_[0:1, :], 0.0), writes=xl_.r)
                    S.dma("sp", xl_[p_lo:np_, :], res_d[tok_lo + p_lo:tok_lo + np_, :], writes=xl_.r)
                    return xl_

                def wdown_tile(ti):
                    t0, s0, sl, j, nt = tiles[ti]
                    xls = []
                    for b in range(4):
                        p_lo = 1 if (j == 0 and b == 0) else 0
                        xls.append(res_load(128, t0 - 1 + b * 128, p_lo))
                    for f in range(2):
                        accs = [nps() for b in range(4)]
                        for v in range(11):
                            S.op("pe", [MM(accs[b][:, :], G[:, v, b * 128:(b + 1) * 128], wDn[:, v, f * 512:(f + 1) * 512],
                                           v == 0, v == 10) for b in range(4)], reads=G.ri(v) + wDn.ri(v),
                                 writes=rr(*accs))
                        for b in range(4):
                            S.op("dve", TTO(xls[b][:, f * 512:(f + 1) * 512], accs[b][:, :],
                                            xls[b][:, f * 512:(f + 1) * 512], ALU.add), reads=accs[b].r + xls[b].ri(f),
                                 writes=xls[b].ri(f))
                    return xls

                def wdown_fin(ti, xls):
                    t0, s0, sl, j, nt = tiles[ti]
                    if final:
                        S.op("dve", MSET(ss4[:, :], 0.0), writes=ss4.r)
                        for b in range(4):
                            S.op("act", ACTF(junk[:, b, :], xls[b][:, :], AF.Square, accum_out=ss4[:, b:b + 1]),
                                 reads=xls[b].r, writes=junk.ri(b) + ss4.r)
                        S.op("act", ACTF(rs4[:, :], ss4[:, :], AF.Sqrt, bias=epsT[:, 0:1], scale=1.0 / D),
                             reads=rr(ss4, epsT), writes=rs4.r)
                        S.op("dve", RECIP(rs4[:, :], rs4[:, :]), reads=rs4.r, writes=rs4.r)
                        for b in range(4):
                            S.op("dve", STT(xls[b][:, :], xls[b][:, :], rs4[:, b:b + 1], fg[:, :], ALU.mult, ALU.mult),
                                 reads=rr(xls[b], rs4, fg), writes=xls[b].r)
                    for b in range(4):
                        p_lo = 1 if (j == 0 and b == 0) else 0
                        tok_lo = t0 - 1 + b * 128
                        S.dma("sp", dst_d[tok_lo + p_lo:tok_lo + 128, :], xls[b][p_lo:128, :], reads=xls[b].r)

                def tail(ti, yc):
                    t0, s0, sl, j, nt = tiles[ti]
                    S.op("dve", TTO(ucol[:, :], wk[:, 0, :], yc[:, :, 0], ALU.mult), reads=rr(wk, yc), writes=ucol.r)
                    S.op("dve", TTO(ucol2[:, :], wk[:, 1, :], yc[:, :, 1], ALU.mult), reads=rr(wk, yc), writes=ucol2.r)
                    S.op("dve", TTO(ucol2[:, :], ucol2[:, :], ucol[:, :], ALU.add), reads=rr(ucol, ucol2), writes=ucol2.r)
                    S.op("act", ACTF(ucol[:, 11:22], ucol2[:, 11:22], AF.Silu), reads=ucol2.r, writes=ucol.r)
                    S.op("dve", TTO(gcol[:, :], ucol2[:, 0:11], ucol[:, 11:22], ALU.mult), reads=rr(ucol, ucol2),
                         writes=gcol.r)
                    tok = s0 + sl - 1
                    xl_ = res_load(1, tok, 0)
                    for f in range(2):
                        pw = nps()
                        S.op("pe", [MM(pw[0:1, :], gcol[:, v:v + 1], wDn[:, v, f * 512:(f + 1) * 512], v == 0, v == 10)
                                    for v in range(11)], reads=rr(gcol, wDn), writes=pw.r)
                        S.op("dve", TTO(xl_[0:1, f * 512:(f + 1) * 512], pw[0:1, :], xl_[0:1, f * 512:(f + 1) * 512],
                                        ALU.add), reads=pw.r + xl_.ri(f), writes=xl_.ri(f))
                    out_finish(xl_, 1, tok, 0)

                def do_norm1(ti):
                    X = xt[ti % 2]
                    norm1(lambda b: X[:, b, :], lambda b: X.r, hn, ssq, rstd, junk)

                def yphase(ti):
                    t0, s0, sl, j, nt = tiles[ti]
                    hT = hTs[ti % 2]
                    yc_old = ycar[j % 2]
                    yc_new = ycar[(j + 1) % 2]
                    if j == 0:
                        S.op("dve", MSET(cf[:, :, :], 0.0), writes=cf.r)
                    else:
                        S.op("dve", TTO(cft[:, :], wk[:, 1, :], yc_old[:, :, 1], ALU.mult), reads=rr(wk, yc_old),
                             writes=cft.r)
                        S.op("dve", TTO(cft2[:, :], wk[:, 0, :], yc_old[:, :, 0], ALU.mult), reads=rr(wk, yc_old),
                             writes=cft2.r)
                        S.op("dve", TTO(cf[:, :, 0], cft[:, :], cft2[:, :], ALU.add), reads=rr(cft, cft2), writes=cf.r)
                        S.op("dve", TTO(cf[:, :, 1], wk[:, 0, :], yc_old[:, :, 1], ALU.mult), reads=rr(wk, yc_old),
                             writes=cf.r)
                    for v in range(11):
                        us = []
                        for vg in (1, 0):
                            ci = vg * 11 + v
                            py = nps()
                            S.op("pe", [MM(py[:, :], wUp[:, kc, vg, v * 128:(v + 1) * 128], hT[:, kc, :], kc == 0, kc == 7)
                                        for kc in range(8)], reads=wUp.ri(vg * 11 + v) + hT.r, writes=py.r)
                            u = ur[nu[0] % 4]
                            nu[0] += 1
                            w0, w1, w2 = (wk[:, k, ci:ci + 1] for k in range(3))
                            S.op("act", [lambda e, u=u, py=py, w2=w2: e.mul(out=u[:, 2:TT], in_=py[:, 2:TT], mul=w2),
                                         ACTF(u[:, 0:1], py[:, 0:1], AF.Identity, bias=cf[:, ci, 0:1], scale=w2),
                                         ACTF(u[:, 1:2], py[:, 1:2], AF.Identity, bias=cf[:, ci, 1:2], scale=w2)],
                                 reads=rr(py, wk, cf), writes=u.r)
                            S.op("act", ACTF(yc_new[:, ci, :], py[:, TT - 2:TT], AF.Identity), reads=py.r,
                                 writes=yc_new.ri(ci))
                            S.op("dve", STT(u[:, 1:TT], py[:, 0:TT - 1], w1, u[:, 1:TT], ALU.mult, ALU.add),
                                 reads=rr(py, wk, u), writes=u.r)
                            S.op("dve", STT(u[:, 2:TT], py[:, 0:TT - 2], w0, u[:, 2:TT], ALU.mult, ALU.add),
                                 reads=rr(py, wk, u), writes=u.r)
                            us.append(u)
                        ug, uv = us
                        sg_ = sgt[v % 2]
                        S.op("act", ACTF(sg_[:, :], ug[:, :], AF.Silu), reads=ug.r, writes=sg_.r)
                        S.op("dve", TTO(G[:, v, :], uv[:, :], sg_[:, :], ALU.mult), reads=rr(uv, sg_), writes=G.ri(v))

                def hstore(ti):
                    t0 = tiles[ti][0]
                    S.dma("sp", HT_d[:, t0:t0 + TT].rearrange("(c p) t -> p c t", p=128), hTs[ti % 2][:, :, :],
                          reads=hTs[ti % 2].r)

                def hload(ti):
                    t0 = tiles[ti][0]
                    S.dma("sp", hTs[ti % 2][:, :, :], HT_d[:, t0:t0 + TT].rearrange("(c p) t -> p c t", p=128),
                          writes=hTs[ti % 2].r)

                if half == 0:
                    load(0)
                    do_norm1(0)
                    norm2(hn, hTs[0], l, V_G2)
                    hstore(0)
                else:
                    hload(0)
                for ti, (t0, s0, sl, j, nt) in enumerate(tiles):
                    nxt = ti + 1 < len(tiles)
                    if nxt:
                        if half == 0:
                            load(ti + 1)
                        else:
                            hload(ti + 1)
                    yphase(ti)
                    if nxt and half == 0:
                        do_norm1(ti + 1)
                    xls = wdown_tile(ti)
                    if nxt and half == 0:
                        norm2(hn, hTs[(ti + 1) % 2], l, V_G2)
                        hstore(ti + 1)
                    wdown_fin(ti, xls)
                    if j == nt - 1:
                        tail(ti, ycar[(j + 1) % 2])
                S.barrier()

        cur = x_d
        stop = False
        for l in range(depth):
            last = (l == depth - 1)
            for name, fn in (("p1", lambda: phase1(l, cur)),
                             ("p2a", lambda: phase2a(l)),
                             ("p2b", lambda: phase2b(l, cur)),
                             ("p3a", lambda: phase3(l, 0, XM_d, PA_d, False)),
                             ("p3b", lambda: phase3(l, 1, PA_d, y_d if last else XL_d, last))):
                fn()
                if upto == f"{name}.{l}":
                    stop = True
                    break
            if stop:
                break
            cur = XL_d
        S.barrier()
        print("instructions:", S.nins, "ops:", S.nops, "engine counts:", S.cnt)
    return nc


def _host_consts(inp):
    f32 = np.float32
    vecs = np.zeros((L, 128, NV), f32)

    def pc(v, n):
        return np.ascontiguousarray(np.asarray(v, f32).reshape(n, 128).T)

    for l in range(L):
        vecs[l, :, V_G1:V_G1 + 8] = pc(inp["norm1_g"][l], 8)
        vecs[l, :, V_G2:V_G2 + 8] = pc(inp["norm2_g"][l], 8)
        for i in range(4):
            vecs[l, :, V_BG + i * 8:V_BG + (i + 1) * 8] = pc(inp["b_gate"][l, i], 8)
        vecs[l, :, V_CAB:V_CAB + 2] = pc(inp["conv_a_b"][l], 2)
        vecs[l, :, V_LNG:V_LNG + 2] = pc(inp["ln_a_g"][l], 2)
        vecs[l, :, V_LNB:V_LNB + 2] = pc(inp["ln_a_b"][l], 2)
        vecs[l, :, V_PSC:V_PSC + 2] = pc(inp["pool_scale"][l], 2)
        for c in range(2):
            for k in range(31):
                vecs[l, :, V_CAW + c * 31 + k] = inp["conv_a_w"][l, k, c * 128:(c + 1) * 128]
            for k in range(3):
                vecs[l, :, V_SCW + c * 3 + k] = inp["sc_w"][l, k, c * 128:(c + 1) * 128]
        for cc in range(44):
            for k in range(3):
                vecs[l, :, V_FCW + cc * 3 + k] = inp["ffn_conv_w"][l, k, cc * 128:(cc + 1) * 128]
                vecs[l, :, V_FCT + k * 44 + cc] = inp["ffn_conv_w"][l, k, cc * 128:(cc + 1) * 128]
    for l in range(L):
        caw = np.asarray(inp["conv_a_w"][l], f32)
        for g in range(8):
            for q in range(8):
                for s_ in range(4):
                    k = 4 * q + s_
                    if k < 31:
                        vecs[l, 32 * s_:32 * s_ + 32, V_WST + g * 8 + q] = caw[k, g * 32:(g + 1) * 32]
    pbd = np.zeros((L, 2, 128, 128), f32)
    for l in range(L):
        for c in range(2):
            pbd[l, c, 0:64, 0:64] = inp["pool_w"][l, 2 * c]
            pbd[l, c, 64:128, 64:128] = inp["pool_w"][l, 2 * c + 1]
    kc = np.arange(64)[:, None]
    qc = np.arange(64)[None, :]
    idx = np.clip(kc - qc + 15, 0, 30)
    rpb = np.asarray(inp["rpb"], f32)
    tbh = np.ascontiguousarray(rpb[:, :, :, idx].transpose(0, 3, 1, 2, 4))
    cstart = np.clip(qc - 8, 0, 48)
    cm = np.where((kc >= cstart) & (kc < cstart + 16), 0.0, -1e30).astype(f32)
    cmf = np.ascontiguousarray(np.broadcast_to(np.concatenate([cm, cm], 0)[:, None, :], (128, 14, 64))).astype(f32)
    icnt = np.zeros((4, 128, 2, TT), f32)
    pos = np.arange(TT)
    for var in range(4):
        first, last = var & 1, (var >> 1) & 1
        for g, w in enumerate(POOL_WINDOWS):
            lo = pos - w // 2
            hi = pos + w // 2 - 1
            if first:
                lo = np.maximum(lo, 0)
            if last:
                hi = np.minimum(hi, TT - 1)
            cnt = (hi - lo + 1).astype(f32)
            c, half = g // 2, g % 2
            icnt[var, half * 64:(half + 1) * 64, c, :] = (1.0 / cnt)[None, :]
    fgb = np.ascontiguousarray(np.broadcast_to(np.asarray(inp["final_g"], f32)[None, :], (128, D)))
    return dict(vecs=vecs, pbd=pbd, tbh=tbh, cmf=cmf, icnt=icnt, fgb=fgb)


def _shared_inputs(inp):
    d = _host_consts(inp)
    for k in ("w_in", "w_o", "w_up", "w_down"):
        d[k] = np.ascontiguousarray(np.asarray(inp[k], np.float32))
    d["w_out_a"] = np.ascontiguousarray(np.asarray(inp["w_out_a"], np.float32))
    d["w_out_b"] = np.ascontiguousarray(np.asarray(inp["w_out_b"], np.float32))
    d["w_out_c"] = np.ascontiguousarray(np.asarray(inp["w_out_c"], np.float32))
    d["w_out_d"] = np.ascontiguousarray(np.asarray(inp["w_out_d"], np.float32))
    return d


def kernel(**inputs):
    xp = np.asarray(inputs["x_prompt"], np.float32)
    xs = np.asarray(inputs["x_sample"], np.float32)
    n = 8
    shared = _shared_inputs(inputs)
    in_maps = []
    for c in range(n):
        xc = np.concatenate([xp[2 * c].reshape(-1, D), xp[2 * c + 1].reshape(-1, D), xs[c].reshape(-1, D)], axis=0)
        m = dict(shared)
        m["x"] = np.ascontiguousarray(xc)
        in_maps.append(m)
    nc = build_nc([4096, 4096, 8192])
    res = run_bass_kernel_spmd(nc, in_maps, core_ids=list(range(n)))
    yp = np.empty_like(xp)
    ys = np.empty_like(xs)
    for c in range(n):
        y = res.results[c]["y"]
        yp[2 * c] = y[0:4096]
        yp[2 * c + 1] = y[4096:8192]
        ys[c] = y[8192:16384]
    return (yp, ys)
```

```python
import numpy as np
from contextlib import ExitStack
import concourse.bass as bass
import concourse.mybir as mybir
from concourse.bass_utils import run_bass_kernel_spmd

F32 = mybir.dt.float32
BF16 = mybir.dt.bfloat16
AF = mybir.ActivationFunctionType
ALU = mybir.AluOpType

D = 1024
DIN = 6400
DFF = 2816
L = 2
TT = 512
GRID_W = 64
POOL_WINDOWS = (2, 4, 8, 16)
EPS = 1e-6
V_G1, V_G2, V_BG, V_CAB, V_LNG, V_LNB, V_PSC, V_CAW, V_SCW, V_FCW = 0, 8, 16, 48, 50, 52, 54, 56, 118, 124
V_FCT = 256
V_WST = 388
NV = 452


def MM(out, lhsT, rhs, start, stop):
    return lambda e: e.matmul(out, lhsT, rhs, start=start, stop=stop)


def TR(out, in_, ident):
    return lambda e: e.transpose(out, in_, ident)


def ACTF(out, in_, func, bias=None, scale=None, accum_out=None):
    kw = {}
    if bias is not None:
        kw["bias"] = bias
    if scale is not None:
        kw["scale"] = scale
    if accum_out is not None:
        kw["accum_out"] = accum_out
    return lambda e: e.activation(out=out, in_=in_, func=func, **kw)


def TTO(out, a, b, op):
    return lambda e: e.tensor_tensor(out=out, in0=a, in1=b, op=op)


def TS(out, a, s1, op0, s2=None, op1=None):
    if op1 is None:
        return lambda e: e.tensor_scalar(out=out, in0=a, scalar1=s1, scalar2=None, op0=op0)
    return lambda e: e.tensor_scalar(out=out, in0=a, scalar1=s1, scalar2=s2, op0=op0, op1=op1)


def STT(out, a, s, b, op0, op1):
    return lambda e: e.scalar_tensor_tensor(out=out, in0=a, scalar=s, in1=b, op0=op0, op1=op1)


def CP(out, a):
    return lambda e: e.tensor_copy(out=out, in_=a)


def MSET(out, v):
    return lambda e: e.memset(out, v)


def RECIP(out, a):
    return lambda e: e.reciprocal(out=out, in_=a)


class Res:
    __slots__ = ("lw", "rd")

    def __init__(self):
        self.lw = None
        self.rd = {}


class Tile:
    def __init__(self, t, nres=1):
        self.t = t
        self.rs = [Res() for _ in range(nres)]

    def __getitem__(self, k):
        return self.t[k]

    @property
    def r(self):
        return self.rs

    def ri(self, *idx):
        return [self.rs[i] for i in idx]


class Sched:
    def __init__(self, nc, es):
        self.nc = nc
        self.eng = {"pe": nc.tensor, "act": nc.scalar, "dve": nc.vector, "pool": nc.gpsimd, "sp": nc.sync}
        self.sem = {k: es.enter_context(nc.semaphore("s_" + k)) for k in self.eng}
        self.cnt = {k: 0 for k in self.eng}
        self.pools = {"ld": (0, 56), "st": (56, 16), "sw": (72, 8)}
        self.dsem = [es.enter_context(nc.semaphore(f"sd{i}")) for i in range(80)]
        self.dcnt = [0] * 80
        self.dnext = {"ld": 0, "st": 0, "sw": 0}
        self.waited = {k: {} for k in self.eng}
        self.nins = 0
        self.nops = 0

    def _semh(self, key):
        return self.sem[key] if isinstance(key, str) else self.dsem[key]

    def _wait(self, eng, key, val):
        w = self.waited[eng]
        if w.get(key, 0) >= val:
            return
        self.eng[eng].wait_ge(self._semh(key), val)
        w[key] = val
        self.nins += 1

    def _deps(self, eng, reads, writes):
        deps = {}
        for r in reads:
            if r.lw is not None:
                k, v = r.lw
                if v > deps.get(k, 0):
                    deps[k] = v
        for w in writes:
            if w.lw is not None:
                k, v = w.lw
                if v > deps.get(k, 0):
                    deps[k] = v
            for k, v in w.rd.items():
                if v > deps.get(k, 0):
                    deps[k] = v
        for k, v in deps.items():
            if k == "pe" and eng == "pe":
                continue
            self._wait(eng, k, v)

    def op(self, eng, fns, reads=(), writes=()):
        self.nops += 1
        self._deps(eng, reads, writes)
        e = self.eng[eng]
        ins = None
        if not isinstance(fns, (list, tuple)):
            fns = [fns]
        for f in fns:
            ins = f(e)
        self.nins += len(fns)
        self.cnt[eng] += 1
        ins.then_inc(self.sem[eng], 1)
        v = self.cnt[eng]
        for r in reads:
            r.rd[eng] = v
        for w in writes:
            w.lw = (eng, v)
            w.rd = {}

    def dma(self, q, out, in_, reads=(), writes=(), slow=False):
        self.nops += 1
        kind = "sw" if q == "pool" else ("st" if len(reads) else "ld")
        base, n = self.pools[kind]
        i = base + self.dnext[kind]
        self.dnext[kind] = (self.dnext[kind] + 1) % n
        if self.dcnt[i]:
            self._wait(q, i, self.dcnt[i])
        self._deps(q, reads, writes)
        if slow:
            self.eng[q].dma_start(out=out, in_=in_, allow_slow_non_contiguous=True).then_inc(self.dsem[i], 16)
        else:
            self.eng[q].dma_start(out=out, in_=in_).then_inc(self.dsem[i], 16)
        self.nins += 1
        self.dcnt[i] += 16
        v = self.dcnt[i]
        for r in reads:
            r.rd[i] = v
        for w in writes:
            w.lw = (i, v)
            w.rd = {}

    def barrier(self):
        for e in self.eng:
            for k in self.eng:
                if self.cnt[k]:
                    self._wait(e, k, self.cnt[k])
            for i, c in enumerate(self.dcnt):
                if c:
                    self._wait(e, i, c)


def rr(*tiles):
    out = []
    for t in tiles:
        out.extend(t.rs)
    return out


def build_nc(seq_lens, depth=L, debug=False, upto=None):
    Ttot = sum(seq_lens)
    seqs = []
    o = 0
    for n in seq_lens:
        seqs.append((o, n))
        o += n
    tiles = []
    for (s0, n) in seqs:
        nt = n // TT
        for j in range(nt):
            tiles.append((s0 + j * TT, s0, n, j, nt))

    nc = bass.Bass("TRN2", target_bir_lowering=False)

    def din(name, shape, dt=F32):
        return nc.dram_tensor(name, list(shape), dt, kind="ExternalInput").ap()

    def dscr(name, shape, dt):
        if debug:
            return nc.dram_tensor(name, list(shape), dt, kind="ExternalOutput").ap()
        return nc.dram_tensor(name, list(shape), dt).ap()

    x_d = din("x", [Ttot, D])
    w_in_d = din("w_in", [L, D, DIN])
    w_out_d = [din("w_out_" + n, [L, 256, D]) for n in "abcd"]
    w_o_d = din("w_o", [L, D, D])
    w_up_d = din("w_up", [L, D, 2 * DFF])
    w_dn_d = din("w_down", [L, DFF, D])
    vecs_d = din("vecs", [L, 128, NV])
    pbd_d = din("pbd", [L, 2, 128, 128])
    tbh_d = din("tbh", [L, 64, 4, 15, 64])
    cmf_d = din("cmf", [128, 14, 64])
    icnt_d = din("icnt", [4, 128, 2, TT])
    fgb_d = din("fgb", [128, D])
    y_d = nc.dram_tensor("y", [Ttot, D], F32, kind="ExternalOutput").ap()

    HT_d = dscr("HT", [D, Ttot], BF16)
    AT_d = dscr("AT", [256, Ttot], BF16)
    XB_d = dscr("XB", [256, Ttot], F32)
    GB_d = dscr("GB", [256, Ttot], F32)
    CX_d = dscr("CX", [256, Ttot], BF16)
    QT_d = dscr("QT", [256, Ttot], BF16)
    KT_d = dscr("KT", [256, Ttot], BF16)
    VT_d = dscr("VT", [Ttot, 512], BF16)
    FT_d = dscr("FT", [D, Ttot], BF16)
    XM_d = dscr("XM", [Ttot, D], F32)
    PA_d = dscr("PA", [Ttot, D], F32)
    XL_d = dscr("XL", [Ttot, D], F32)

    with ExitStack() as es:
        S = Sched(nc, es)

        uid = [0]

        def sb(stack, name, shape, dt, nres=1):
            uid[0] += 1
            return Tile(stack.enter_context(nc.sbuf_tensor(f"sb{uid[0]}_{name}", list(shape), dt)), nres)

        psum = [Tile(es.enter_context(nc.psum_tensor(f"ps{i}", [128, 512], F32))) for i in range(8)]
        pidx = [0]
        pring = [8]

        def nps():
            p = psum[pidx[0] % pring[0]]
            pidx[0] += 1
            return p

        identF = sb(es, "identF", [128, 128], F32)
        identB = sb(es, "identB", [128, 128], BF16)
        onesF = sb(es, "onesF", [128, 128], F32)
        epsT = sb(es, "epsT", [128, 1], F32)
        vecs = sb(es, "vecs", [128, L, NV], F32)

        S.op("pool", MSET(identF[:, :], 0.0), writes=identF.r)
        S.op("pool", lambda e: e.affine_select(out=identF[:, :], in_=identF[:, :], pattern=[[-1, 128]],
                                               compare_op=ALU.not_equal, fill=1.0, base=0, channel_multiplier=1),
             reads=identF.r, writes=identF.r)
        S.op("pool", CP(identB[:, :], identF[:, :]), reads=identF.r, writes=identB.r)
        S.op("pool", MSET(onesF[:, :], 1.0), writes=onesF.r)
        S.op("pool", MSET(epsT[:, :], EPS), writes=epsT.r)
        for l in range(L):
            S.dma("sp", vecs[:, l, :], vecs_d[l, :, :], writes=vecs.r)

        def vcol(l, c):
            return vecs[:, l, c:c + 1]

        def wload(dst_ap, src_ap, dst_tile):
            S.dma("pool", dst_ap, src_ap, writes=dst_tile.r)

        def norm1(xt_ap_of_b, x_res, hn, ssq, rstd, junk):
            S.op("dve", MSET(ssq[:, :], 0.0), writes=ssq.r)
            for b in range(4):
                S.op("act", ACTF(junk[:, b, :], xt_ap_of_b(b), AF.Square, accum_out=ssq[:, b:b + 1]),
                     reads=x_res(b), writes=junk.ri(b) + ssq.r)
            S.op("act", ACTF(rstd[:, :], ssq[:, :], AF.Sqrt, bias=epsT[:, 0:1], scale=1.0 / D),
                 reads=rr(ssq, epsT), writes=rstd.r)
            S.op("dve", RECIP(rstd[:, :], rstd[:, :]), reads=rstd.r, writes=rstd.r)
            for b in range(4):
                if b % 2 == 0:
                    S.op("dve", TS(hn[:, b, :], xt_ap_of_b(b), rstd[:, b:b + 1], ALU.mult),
                         reads=x_res(b) + rstd.r, writes=hn.ri(b))
                else:
                    S.op("act", lambda e, b=b: e.mul(out=hn[:, b, :], in_=xt_ap_of_b(b), mul=rstd[:, b:b + 1]),
                         reads=x_res(b) + rstd.r, writes=hn.ri(b))

        def norm2(hn, hT, l, gcol0):
            for c in range(8):
                pt = nps()
                ptb = pt[:, :].bitcast(BF16)
                S.op("pe", [TR(ptb[:, b * 128:(b + 1) * 128], hn[:, b, c * 128:(c + 1) * 128], identB[:, :])
                            for b in range(4)], reads=rr(hn, identB), writes=pt.r)
                if c % 2 == 0:
                    S.op("act", lambda e, c=c, ptb=ptb: e.mul(out=hT[:, c, :], in_=ptb[:, 0:512], mul=vcol(l, gcol0 + c)),
                         reads=rr(pt, vecs), writes=hT.ri(c))
                else:
                    S.op("dve", TS(hT[:, c, :], ptb[:, 0:512], vcol(l, gcol0 + c), ALU.mult),
                         reads=rr(pt, vecs), writes=hT.ri(c))

        def phase1(l, xin_d):
            with ExitStack() as st:
                wA = sb(st, "wA", [128, 8, 2304], BF16, 18)
                for jc in (2, 0, 3, 1, 4, 6, 5, 7, 8, 10, 9, 11, 12, 14, 13, 15, 16, 17):
                    S.dma("pool", wA[:, :, jc * 128:(jc + 1) * 128],
                          w_in_d[l, :, jc * 128:(jc + 1) * 128].rearrange("(c p) n -> p c n", p=128), writes=wA.ri(jc))
                xt = [sb(st, f"xt{i}", [128, 4, D], F32) for i in range(2)]
                hn = sb(st, "hn", [128, 4, D], BF16, 4)
                junk = sb(st, "junk", [128, 4, D], BF16, 4)
                ssq = sb(st, "ssq", [128, 4], F32)
                rstd = sb(st, "rstd", [128, 4], F32)
                hT = [sb(st, f"hT{i}", [128, 8, TT], BF16, 8) for i in range(2)]
                aT = [sb(st, f"aT{i}", [128, 2, TT], BF16, 2) for i in range(2)]
                xbT = [sb(st, f"xbT{i}", [128, 2, TT], F32, 2) for i in range(2)]
                gbT = [sb(st, f"gbT{i}", [128, 2, TT], F32, 2) for i in range(2)]
                cxT = [sb(st, f"cxT{i}", [128, 2, TT], BF16, 2) for i in range(2)]
                qT = [sb(st, f"qT{i}", [128, 2, TT], BF16, 2) for i in range(2)]
                kT = [sb(st, f"kT{i}", [128, 2, TT], BF16, 2) for i in range(2)]
                vT = [sb(st, f"vT{i}", [128, 4, 512], BF16, 4) for i in range(2)]
                for i in range(2):
                    S.op("dve", MSET(vT[i][:, :, :], 0.0), writes=vT[i].r)
                sg = [sb(st, f"sg{i}", [128, TT], F32) for i in range(4)]

                def load(ti):
                    t0 = tiles[ti][0]
                    S.dma("sp", xt[ti % 2][:, :, :], xin_d[t0:t0 + TT, :].rearrange("(b p) d -> p b d", p=128),
                          writes=xt[ti % 2].r)

                def do_norm1(ti):
                    X = xt[ti % 2]
                    norm1(lambda b: X[:, b, :], lambda b: X.r, hn, ssq, rstd, junk)

                load(0)
                do_norm1(0)
                for ti, (t0, s0, sl, j, nt) in enumerate(tiles):
                    pb = ti % 2
                    if ti + 1 < len(tiles):
                        load(ti + 1)
                    norm2(hn, hT[pb], l, V_G1)
                    H = hT[pb]
                    S.dma("sp", HT_d[:, t0:t0 + TT].rearrange("(c p) t -> p c t", p=128), H[:, :, :], reads=H.r)

                    def proj(jc):
                        pz = nps()
                        S.op("pe", [MM(pz[:, :], wA[:, kc, jc * 128:(jc + 1) * 128], H[:, kc, :], kc == 0, kc == 7)
                                    for kc in range(8)], reads=wA.ri(jc) + H.r, writes=pz.r)
                        return pz

                    for c in range(2):
                        pz = proj(2 + c)
                        S.op("act", ACTF(sg[c][:, :], pz[:, :], AF.Sigmoid), reads=pz.r, writes=sg[c].r)
                        pz = proj(c)
                        S.op("dve", TTO(aT[pb][:, c, :], pz[:, :], sg[c][:, :], ALU.mult), reads=rr(pz, sg[c]),
                             writes=aT[pb].ri(c))
                    for c in range(2):
                        pz = proj(4 + c)
                        S.op("act", ACTF(xbT[pb][:, c, :], pz[:, :], AF.Identity), reads=pz.r, writes=xbT[pb].ri(c))
                        pz = proj(6 + c)
                        S.op("act", ACTF(gbT[pb][:, c, :], pz[:, :], AF.Identity), reads=pz.r, writes=gbT[pb].ri(c))
                    if ti + 1 < len(tiles):
                        do_norm1(ti + 1)
                    for c in range(2):
                        pz = proj(8 + c)
                        S.op("act", ACTF(sg[2 + c][:, :], pz[:, :], AF.Identity), reads=pz.r, writes=sg[2 + c].r)
                        pz = proj(10 + c)
                        S.op("dve", TTO(cxT[pb][:, c, :], pz[:, :], sg[2 + c][:, :], ALU.mult),
                             reads=rr(pz, sg[2 + c]), writes=cxT[pb].ri(c))
                    for c in range(2):
                        pz = proj(12 + c)
                        S.op("act", ACTF(qT[pb][:, c, :], pz[:, :], AF.Identity, scale=0.125), reads=pz.r,
                             writes=qT[pb].ri(c))
                        pz = proj(14 + c)
                        S.op("dve", CP(kT[pb][:, c, :], pz[:, :]), reads=pz.r, writes=kT[pb].ri(c))
                    for b in range(4):
                        pv = nps()
                        S.op("pe", [MM(pv[:, 0:256], H[:, kc, b * 128:(b + 1) * 128], wA[:, kc, 2048:2304], kc == 0, kc == 7)
                                    for kc in range(8)], reads=wA.ri(16, 17) + H.r, writes=pv.r)
                        vdst = vT[pb][:, b, :].rearrange("p (h2 y d) -> p h2 y d", h2=2, y=4)[:, :, 0:4:3, :]
                        vsrc = pv[:, 0:256].rearrange("p (h2 hh d) -> p h2 hh d", h2=2, hh=2)
                        if b % 2:
                            S.op("dve", CP(vdst, vsrc), reads=pv.r, writes=vT[pb].ri(b))
                        else:
                            S.op("act", ACTF(vdst, vsrc, AF.Identity), reads=pv.r, writes=vT[pb].ri(b))
                    for (dd, tl) in ((AT_d, aT), (XB_d, xbT), (GB_d, gbT), (CX_d, cxT), (QT_d, qT), (KT_d, kT)):
                        S.dma("sp", dd[:, t0:t0 + TT].rearrange("(c p) t -> p c t", p=128), tl[pb][:, :, :],
                              reads=tl[pb].r)
                    S.dma("sp", VT_d[t0:t0 + TT, :].rearrange("(b p) f -> p b f", p=128), vT[pb][:, :, :],
                          reads=vT[pb].r)
                S.barrier()

        def phase2a(l):
            with ExitStack() as st:
                L4 = sb(st, "L4", [128, 8, 8, 32], BF16, 64)
                identS = sb(st, "identS", [128, 32], F32)
                for s_ in range(4):
                    S.op("dve", CP(identS[32 * s_:32 * s_ + 32, :], identF[32 * s_:32 * s_ + 32, 32 * s_:32 * s_ + 32]),
                         reads=identF.r, writes=identS.r)
                for g in range(8):
                    for q in range(8):
                        S.op("dve", TS(L4[:, g, q, :], identS[:, :], vcol(l, V_WST + g * 8 + q), ALU.mult),
                             reads=rr(identS, vecs), writes=L4.ri(g * 8 + q))
                diagC = sb(st, "diagC", [128, 2, 3, 128], BF16, 6)
                pbd = sb(st, "pbd", [128, 2, 128], BF16)
                Tb = sb(st, "Tb", [128, 4, 14, 64], F32, 4)
                onesP = sb(st, "onesP", [128, 2, 128], BF16)
                for c in range(2):
                    for k in range(3):
                        S.op("dve", TS(diagC[:, c, k, :], identF[:, :], vcol(l, V_SCW + c * 3 + k), ALU.mult),
                             reads=rr(identF, vecs), writes=diagC.ri(c * 3 + k))
                    wload(pbd[:, c, :], pbd_d[l, c, :, :], pbd)
                S.op("pool", MSET(onesP[:, :, :], 0.0), writes=onesP.r)
                S.op("pool", MSET(onesP[:, 0, 0:64], 1.0), writes=onesP.r)
                S.op("pool", MSET(onesP[:, 1, 64:128], 1.0), writes=onesP.r)
                with ExitStack() as st2:
                    cmf = sb(st2, "cmf", [128, 14, 64], F32)
                    S.dma("sp", cmf[:, :, :], cmf_d[:, :, :], writes=cmf.r)
                    S.dma("sp", Tb[0:64, :, :, :], tbh_d[l, :, :, 0:14, :], writes=Tb.r)
                    S.dma("sp", Tb[64:128, :, :, :], tbh_d[l, :, :, 1:15, :], writes=Tb.r)
                    for h in range(4):
                        S.op("dve", TTO(Tb[:, h, :, :], Tb[:, h, :, :], cmf[:, :, :], ALU.add), reads=Tb.ri(h) + cmf.r,
                             writes=Tb.ri(h))
                    S.barrier()

                xbH = [sb(st, f"xbH{i}", [128, 2, TT + 16], F32, 3) for i in range(2)]
                cxH = [sb(st, f"cxH{i}", [128, 2, TT + 2], BF16, 3) for i in range(2)]
                gbH = [sb(st, f"gbH{i}", [128, 2, TT], F32) for i in range(2)]
                aS = [sb(st, f"aS{i}", [128, 8, TT + 28], BF16, 32) for i in range(2)]
                qH = [sb(st, f"qH{i}", [128, 2, TT], BF16) for i in range(2)]
                kH = [sb(st, f"kH{i}", [128, 2, 15 * 64], BF16) for i in range(2)]
                icn = [sb(st, f"icn{i}", [128, 2, TT], F32) for i in range(2)]
                Vpd = [sb(st, f"Vpad{i}", [128, 14, 4, 128], BF16, 14) for i in range(2)]
                PT = [sb(st, f"PT{i}", [128, TT], BF16) for i in range(16)]
                sbs = [sb(st, f"sbs{i}", [128, TT], F32) for i in range(3)]
                cb = sb(st, "cb", [128, 2, TT], F32, 2)
                sq = sb(st, "sq", [128, 2, TT], F32, 2)
                mu = sb(st, "mu", [128, TT], F32)
                msq = sb(st, "msq", [128, TT], F32)
                rsd = sb(st, "rsd", [128, TT], F32)
                yln = sb(st, "yln", [128, 2, TT], F32, 2)
                s2 = sb(st, "s2", [128, 2, TT + 16], F32, 2)
                s4 = sb(st, "s4", [128, 2, TT + 16], F32, 2)
                s8 = sb(st, "s8", [128, TT + 16], F32)
                s16 = sb(st, "s16", [128, TT + 16], F32)
                pmean = sb(st, "pmean", [128, 2, TT], F32, 4)
                ppd = [sb(st, f"pp{i}", [128, 2, TT], BF16, 4) for i in range(2)]
                rec = [sb(st, f"rec{i}", [128, TT], F32) for i in range(2)]
                Fo = [sb(st, f"Fo{i}", [128, 8, TT], BF16, 8) for i in range(2)]
                for i in range(2):
                    S.op("dve", MSET(aS[i][:, :, :], 0.0), writes=aS[i].r)
                pring[0] = 6

                def rowinfo(ti):
                    t0, s0, sl, j, nt = tiles[ti]
                    rows = sl // GRID_W
                    r0 = j * 8
                    rs = [min(max(r0 + q - 4, 0), rows - 8) for q in range(8)]
                    lo = rs[0]
                    hi = rs[7] + 8
                    return rows, r0, rs, lo, hi

                def halo_load(dst, dd, ti, hl, hr):
                    t0, s0, sl, j, nt = tiles[ti]
                    S.dma("sp", dst[:, :, hl:hl + TT], dd[:, t0:t0 + TT].rearrange("(c p) t -> p c t", p=128),
                          writes=dst.ri(0))
                    if hl:
                        if j == 0:
                            S.op("pool", MSET(dst[:, :, 0:hl], 0.0), writes=dst.ri(1))
                        else:
                            S.dma("sp", dst[:, :, 0:hl], dd[:, t0 - hl:t0].rearrange("(c p) t -> p c t", p=128),
                                  writes=dst.ri(1), slow=(hl == 1))
                    if hr:
                        if j == nt - 1:
                            S.op("pool", MSET(dst[:, :, hl + TT:hl + TT + hr], 0.0), writes=dst.ri(2))
                        else:
                            S.dma("sp", dst[:, :, hl + TT:hl + TT + hr],
                                  dd[:, t0 + TT:t0 + TT + hr].rearrange("(c p) t -> p c t", p=128), writes=dst.ri(2),
                                  slow=(hr == 1))

                def load(ti):
                    t0, s0, sl, j, nt = tiles[ti]
                    pb = ti % 2
                    if j == 0:
                        S.op("pool", MSET(aS[pb][:, :, 0:15], 0.0), writes=aS[pb].r)
                    if j == nt - 1:
                        S.op("pool", MSET(aS[pb][:, :, TT + 12:TT + 28], 0.0), writes=aS[pb].r)
                    for g in range(8):
                        for s_ in range(4):
                            c_lo = (15 - s_) if j == 0 else 0
                            c_hi = (TT + 15 - s_) if j == nt - 1 else TT + 28
                            tk = t0 - 15 + s_
                            S.dma("sp", aS[pb][32 * s_:32 * s_ + 32, g, c_lo:c_hi],
                                  AT_d[g * 32:(g + 1) * 32, tk + c_lo:tk + c_hi], writes=aS[pb].ri(g * 4 + s_))
                    halo_load(cxH[pb], CX_d, ti, 1, 1)
                    halo_load(qH[pb], QT_d, ti, 0, 0)
                    rows, r0, rs, lo, hi = rowinfo(ti)
                    k0 = s0 + lo * 64
                    nk = (hi - lo) * 64
                    S.dma("sp", kH[pb][:, :, 0:nk], KT_d[:, k0:k0 + nk].rearrange("(c p) t -> p c t", p=128),
                          writes=kH[pb].r)
                    ns = hi - lo - 1
                    for s in range(ns):
                        ks = k0 + s * 64
                        S.dma("sp", Vpd[pb][:, s, :, :].rearrange("p h c -> p (h c)"), VT_d[ks:ks + 128, :],
                              writes=Vpd[pb].ri(s))
                    halo_load(xbH[pb], XB_d, ti, 8, 8)
                    halo_load(gbH[pb], GB_d, ti, 0, 0)
                    var = (1 if j == 0 else 0) + (2 if j == nt - 1 else 0)
                    S.dma("sp", icn[pb][:, :, :], icnt_d[var, :, :, :], writes=icn[pb].r)

                def p2a_tail(F, pp_, t0):
                    pm = nps()
                    S.op("pe", [MM(pm[:, :], onesF[:, :], cb[:, c, :], c == 0, c == 1) for c in range(2)],
                         reads=rr(onesF, cb), writes=pm.r)
                    pq = nps()
                    S.op("pe", [MM(pq[:, :], onesF[:, :], sq[:, c, :], c == 0, c == 1) for c in range(2)],
                         reads=rr(onesF, sq), writes=pq.r)
                    S.op("dve", TS(mu[:, :], pm[:, :], 1.0 / 256, ALU.mult), reads=pm.r, writes=mu.r)
                    S.op("dve", TTO(msq[:, :], mu[:, :], mu[:, :], ALU.mult), reads=mu.r, writes=msq.r)
                    S.op("dve", STT(rsd[:, :], pq[:, :], 1.0 / 256, msq[:, :], ALU.mult, ALU.subtract),
                         reads=rr(pq, msq), writes=rsd.r)
                    S.op("act", ACTF(rsd[:, :], rsd[:, :], AF.Ln, bias=epsT[:, 0:1]), reads=rr(rsd, epsT),
                         writes=rsd.r)
                    S.op("act", ACTF(rsd[:, :], rsd[:, :], AF.Exp, scale=-0.5), reads=rsd.r, writes=rsd.r)
                    for c in range(2):
                        S.op("dve", TTO(yln[:, c, :], cb[:, c, :], mu[:, :], ALU.subtract), reads=cb.ri(c) + mu.r,
                             writes=yln.ri(c))
                        S.op("dve", TTO(yln[:, c, :], yln[:, c, :], rsd[:, :], ALU.mult), reads=yln.ri(c) + rsd.r,
                             writes=yln.ri(c))
                        S.op("act", ACTF(F[:, c, :], yln[:, c, :], AF.Silu, bias=vcol(l, V_LNB + c),
                                         scale=vcol(l, V_LNG + c)), reads=yln.ri(c) + vecs.r, writes=F.ri(c))
                    for c in range(2):
                        pc = nps()
                        S.op("pe", [MM(pc[:, :], pbd[:, c, :], pp_[:, c, :], True, True)],
                             reads=pbd.r + pp_.ri(2 * c, 2 * c + 1), writes=pc.r)
                        S.op("act", lambda e, c=c, pc=pc: e.mul(out=F[:, 2 + c, :], in_=pc[:, :], mul=vcol(l, V_PSC + c)),
                             reads=rr(pc, vecs), writes=F.ri(2 + c))
                    S.dma("sp", FT_d[:, t0:t0 + TT].rearrange("(c p) t -> p c t", p=128), F[:, :, :], reads=F.r)

                load(0)
                for ti, (t0, s0, sl, j, nt) in enumerate(tiles):
                    pb = ti % 2
                    if ti + 1 < len(tiles):
                        load(ti + 1)
                    rows, r0, rs, lo, hi = rowinfo(ti)
                    ns = hi - lo - 1
                    pp = ppd[pb]
                    XBH, CXH, GBH, QH, KH, Vpad, ICN, F = xbH[pb], cxH[pb], gbH[pb], qH[pb], kH[pb], Vpd[pb], icn[pb], Fo[pb]
                    W = TT + 16
                    for c in range(2):
                        S.op("pool", TTO(s2[:, c, 1:W - 1], XBH[:, c, 0:W - 2], XBH[:, c, 1:W - 1], ALU.add),
                             reads=XBH.r, writes=s2.ri(c))
                        S.op("pool", TTO(s4[:, c, 2:W - 2], s2[:, c, 1:W - 3], s2[:, c, 3:W - 1], ALU.add),
                             reads=s2.ri(c), writes=s4.ri(c))
                    S.op("pool", TTO(s8[:, 4:W - 4], s4[:, 1, 2:W - 6], s4[:, 1, 6:W - 2], ALU.add), reads=s4.ri(1),
                         writes=s8.r)
                    S.op("pool", TTO(s16[:, 8:W - 8], s8[:, 4:W - 12], s8[:, 12:W - 4], ALU.add), reads=s8.r,
                         writes=s16.r)
                    srcs = [(s2, 0, 0, 64), (s4, 0, 64, 128), (s8, None, 0, 64), (s16, None, 64, 128)]
                    for g, (stl, cidx, p0, p1) in enumerate(srcs):
                        c = g // 2
                        src = stl[p0:p1, cidx, 8:8 + TT] if cidx is not None else stl[p0:p1, 8:8 + TT]
                        S.op("pool", TTO(pmean[p0:p1, c, :], src, ICN[p0:p1, c, :], ALU.mult), reads=rr(stl, ICN),
                             writes=pmean.ri(g))
                        S.op("pool", TTO(pp[p0:p1, c, :], pmean[p0:p1, c, :], XBH[p0:p1, c, 8:8 + TT], ALU.subtract),
                             reads=pmean.ri(g) + XBH.r, writes=pp.ri(g))
                    pcA = [psum[6], psum[7]]
                    conv_sub = [(c, q) for c in range(2) for q in range(8)]
                    AS = aS[pb]
                    sbi = 0
                    PTs = {}
                    n_it = 0
                    for cpair in range(2):
                        for i in range(4):
                            for hh in range(2):
                                h = cpair * 2 + hh
                                hb = hh * 64
                                pS = nps()
                                mms = []
                                for q in range(8):
                                    kc0 = (rs[q] + 2 * i - lo) * 64
                                    mms.append(MM(pS[:, q * 64:(q + 1) * 64], KH[hb:hb + 64, cpair, kc0:kc0 + 128],
                                                  QH[hb:hb + 64, cpair, q * 64:(q + 1) * 64], True, True))
                                S.op("pe", mms, reads=rr(KH, QH), writes=pS.r)
                                sbt = sbs[sbi % 3]
                                sbi += 1
                                mis = [rs[q] + 2 * i - (r0 + q) + 7 for q in range(8)]
                                q = 0
                                while q < 8:
                                    q2 = q
                                    while q2 + 1 < 8 and mis[q2 + 1] == mis[q]:
                                        q2 += 1
                                    n = q2 - q + 1
                                    S.op("dve", TTO(sbt[:, q * 64:(q2 + 1) * 64].rearrange("p (a b) -> p a b", b=64),
                                                    pS[:, q * 64:(q2 + 1) * 64].rearrange("p (a b) -> p a b", b=64),
                                                    Tb[:, h, mis[q]:mis[q] + 1, :].to_broadcast([128, n, 64]), ALU.add),
                                         reads=rr(pS, Tb), writes=sbt.r)
                                    q = q2 + 1
                                ptile = PT[n_it]
                                S.op("act", ACTF(ptile[:, :], sbt[:, :], AF.Exp), reads=sbt.r, writes=ptile.r)
                                PTs[(cpair, i, hh)] = ptile
                                c, q = conv_sub[n_it]
                                S.op("pe", [(lambda e, c=c, q=q, jj=jj: e.matmul(
                                    pcA[c][32 * jj:32 * jj + 32, :], L4[:, 4 * c + jj, q, :],
                                    AS[:, 4 * c + jj, 4 * q:4 * q + TT], start=(q == 0), stop=(q == 7),
                                    tile_position=(0, 32 * jj))) for jj in range(4)],
                                     reads=rr(L4, AS), writes=pcA[c].r)
                                n_it += 1
                    for c in range(2):
                        pc = nps()
                        S.op("pe", [MM(pc[:, :], diagC[:, c, k, :], CXH[:, c, k:k + TT], k == 0, k == 2)
                                    for k in range(3)], reads=rr(diagC, CXH), writes=pc.r)
                        S.op("dve", TTO(F[:, 4 + c, :], pc[:, :], GBH[:, c, :], ALU.mult), reads=rr(pc, GBH),
                             writes=F.ri(4 + c))
                    if ti > 0:
                        p2a_tail(Fo[(ti - 1) % 2], ppd[(ti - 1) % 2], tiles[ti - 1][0])
                    for c in range(2):
                        S.op("act", ACTF(cb[:, c, :], pcA[c][:, :], AF.Identity, bias=vcol(l, V_CAB + c)),
                             reads=rr(pcA[c], vecs), writes=cb.ri(c))
                        S.op("act", ACTF(sq[:, c, :], cb[:, c, :], AF.Square), reads=cb.ri(c), writes=sq.ri(c))
                    for cpair in range(2):
                        pts = [PTs[(cpair, i, hh)] for hh in range(2) for i in range(4)]
                        po = nps()
                        mms = []
                        for q in range(8):
                            for i in range(4):
                                s = rs[q] + 2 * i - lo
                                for hh in range(2):
                                    h = cpair * 2 + hh
                                    o = hh * 64
                                    mms.append(lambda e, q=q, i=i, s=s, h=h, hh=hh, o=o, po=po: e.matmul(
                                        po[o:o + 64, q * 64:(q + 1) * 64], Vpad[:, s, h, o:o + 64],
                                        PTs[(cpair, i, hh)][:, q * 64:(q + 1) * 64], start=(i == 0), stop=(i == 3),
                                        tile_position=(0, o)))
                        S.op("pe", mms, reads=rr(Vpad, *pts), writes=po.r)
                        pd = nps()
                        mms = []
                        for i in range(4):
                            for hh in range(2):
                                o = hh * 64
                                mms.append(lambda e, i=i, hh=hh, o=o, pd=pd: e.matmul(
                                    pd[o:o + 64, :], onesP[:, hh, o:o + 64], PTs[(cpair, i, hh)][:, :],
                                    start=(i == 0), stop=(i == 3), tile_position=(0, o)))
                        S.op("pe", mms, reads=rr(onesP, *pts), writes=pd.r)
                        S.op("act", ACTF(rec[cpair][:, :], pd[:, :], AF.Ln), reads=pd.r, writes=rec[cpair].r)
                        S.op("act", ACTF(rec[cpair][:, :], rec[cpair][:, :], AF.Exp, scale=-1.0), reads=rec[cpair].r,
                             writes=rec[cpair].r)
                        S.op("dve", TTO(F[:, 6 + cpair, :], po[:, :], rec[cpair][:, :], ALU.mult),
                             reads=rr(po, rec[cpair]), writes=F.ri(6 + cpair))
                if tiles:
                    tl = tiles[-1]
                    p2a_tail(Fo[(len(tiles) - 1) % 2], ppd[(len(tiles) - 1) % 2], tl[0])
                S.barrier()
                pring[0] = 8

        def phase2b(l, xin_d):
            with ExitStack() as st:
                wG = sb(st, "wG", [128, 8, 4096], BF16, 32)
                wOut = sb(st, "wOut", [128, 4, 2, D], BF16, 8)
                wO = sb(st, "wO", [128, 8, D], BF16, 8)
                for i in range(4):
                    for kc in range(2):
                        S.dma("pool", wOut[:, i, kc, :], w_out_d[i][l, kc * 128:(kc + 1) * 128, :],
                              writes=wOut.ri(i * 2 + kc))
                for c in range(8):
                    for i in range(4):
                        c0 = i * D + c * 128
                        S.dma("pool", wG[:, :, c0:c0 + 128],
                              w_in_d[l, :, 2304 + c0:2304 + c0 + 128].rearrange("(c p) n -> p c n", p=128),
                              writes=wG.ri(i * 8 + c))
                for kc in range(8):
                    S.dma("pool", wO[:, kc, :], w_o_d[l, kc * 128:(kc + 1) * 128, :], writes=wO.ri(kc))
                hT = [sb(st, f"hTb{i}", [128, 8, TT], BF16) for i in range(2)]
                Fi = [sb(st, f"Fi{i}", [128, 8, TT], BF16) for i in range(2)]
                xblk = [sb(st, f"xblk{i}", [128, D], F32) for i in range(2)]
                oblk = [sb(st, f"oblk{i}", [128, D], F32, 2) for i in range(2)]
                sgr = [sb(st, f"sgr{i}", [128, TT], F32) for i in range(3)]
                tmr = [sb(st, f"tmr{i}", [128, TT], F32) for i in range(3)]
                Mc = [sb(st, f"Mc{i}", [128, TT], F32) for i in range(2)]
                Mb = sb(st, "Mb", [128, 8, TT], BF16, 8)

                def load(ti):
                    t0 = tiles[ti][0]
                    pb = ti % 2
                    S.dma("sp", hT[pb][:, :, :], HT_d[:, t0:t0 + TT].rearrange("(c p) t -> p c t", p=128),
                          writes=hT[pb].r)
                    S.dma("sp", Fi[pb][:, :, :], FT_d[:, t0:t0 + TT].rearrange("(c p) t -> p c t", p=128),
                          writes=Fi[pb].r)

                load(0)
                nsg = 0
                nb = 0
                for ti, (t0, s0, sl, j, nt) in enumerate(tiles):
                    pb = ti % 2
                    if ti + 1 < len(tiles):
                        load(ti + 1)
                    H, F = hT[pb], Fi[pb]
                    for c in range(8):
                        mc = Mc[c % 2]
                        for i in range(4):
                            pg = nps()
                            S.op("pe", [MM(pg[:, :], wG[:, kc, i * D + c * 128:i * D + (c + 1) * 128], H[:, kc, :],
                                           kc == 0, kc == 7) for kc in range(8)], reads=wG.ri(i * 8 + c) + H.r,
                                 writes=pg.r)
                            pbr = nps()
                            S.op("pe", [MM(pbr[:, :], wOut[:, i, kc, c * 128:(c + 1) * 128], F[:, i * 2 + kc, :],
                                           kc == 0, kc == 1) for kc in range(2)],
                                 reads=wOut.ri(i * 2, i * 2 + 1) + F.r, writes=pbr.r)
                            sgt = sgr[nsg % 3]
                            S.op("act", ACTF(sgt[:, :], pg[:, :], AF.Sigmoid, bias=vcol(l, V_BG + i * 8 + c)),
                                 reads=rr(pg, vecs), writes=sgt.r)
                            if i == 0:
                                S.op("dve", TTO(mc[:, :], pbr[:, :], sgt[:, :], ALU.mult), reads=rr(pbr, sgt),
                                     writes=mc.r)
                            else:
                                tm = tmr[nsg % 3]
                                S.op("dve", TTO(tm[:, :], pbr[:, :], sgt[:, :], ALU.mult), reads=rr(pbr, sgt),
                                     writes=tm.r)
                                if i < 3:
                                    S.op("dve", TTO(mc[:, :], mc[:, :], tm[:, :], ALU.add), reads=rr(mc, tm),
                                         writes=mc.r)
                                else:
                                    S.op("dve", TTO(Mb[:, c, :], mc[:, :], tm[:, :], ALU.add), reads=rr(mc, tm),
                                         writes=Mb.ri(c))
                            nsg += 1
                    for b in range(4):
                        xb_ = xblk[nb % 2]
                        ob_ = oblk[nb % 2]
                        nb += 1
                        S.dma("sp", xb_[:, :], xin_d[t0 + b * 128:t0 + (b + 1) * 128, :], writes=xb_.r)
                        for f in range(2):
                            po = nps()
                            S.op("pe", [MM(po[:, :], Mb[:, kc, b * 128:(b + 1) * 128], wO[:, kc, f * 512:(f + 1) * 512],
                                           kc == 0, kc == 7) for kc in range(8)], reads=rr(Mb, wO), writes=po.r)
                            S.op("dve", TTO(ob_[:, f * 512:(f + 1) * 512], po[:, :], xb_[:, f * 512:(f + 1) * 512], ALU.add),
                                 reads=rr(po, xb_), writes=ob_.ri(f))
                        S.dma("sp", XM_d[t0 + b * 128:t0 + (b + 1) * 128, :], ob_[:, :], reads=ob_.r)
                S.barrier()

        def phase3(l, half, res_d, dst_d, final):
            v0 = half * 11
            with ExitStack() as st:
                wUp = sb(st, "wUp", [128, 8, 2, 11 * 128], BF16, 22)
                wDn = sb(st, "wDn", [128, 11, D], BF16, 11)
                for v in range(11):
                    for vg in (1, 0):
                        c0 = vg * DFF + (v0 + v) * 128
                        S.dma("pool", wUp[:, :, vg, v * 128:(v + 1) * 128],
                              w_up_d[l, :, c0:c0 + 128].rearrange("(c p) n -> p c n", p=128), writes=wUp.ri(vg * 11 + v))
                for v in range(11):
                    S.dma("pool", wDn[:, v, :], w_dn_d[l, (v0 + v) * 128:(v0 + v + 1) * 128, :], writes=wDn.ri(v))
                fg = sb(st, "fg", [128, D], F32)
                if final:
                    S.dma("sp", fg[:, :], fgb_d[:, :], writes=fg.r)
                xt = [sb(st, f"x3{i}", [128, 4, D], F32) for i in range(2)]
                hn = sb(st, "hn3", [128, 4, D], BF16, 4)
                junk = sb(st, "junk3", [128, 4, D], BF16, 4)
                ss4 = sb(st, "ss4", [128, 4], F32)
                rs4 = sb(st, "rs4", [128, 4], F32)
                ssq = sb(st, "ssq3", [128, 4], F32)
                rstd = sb(st, "rstd3", [128, 4], F32)
                hTs = [sb(st, f"hT3{i}", [128, 8, TT], BF16, 8) for i in range(2)]
                G = sb(st, "G3", [128, 11, TT], BF16, 11)
                ur = [sb(st, f"u3{i}", [128, TT], F32) for i in range(6)]
                sgt = [sb(st, f"sg3{i}", [128, TT], F32) for i in range(4)]
                ycar = [sb(st, f"ycar{i}", [128, 22, 2], F32, 22) for i in range(2)]
                ss2 = sb(st, "ss2", [128, 2], F32)
                rs2 = sb(st, "rs2", [128, 2], F32)
                ucol = sb(st, "ucol", [128, 22], F32)
                ucol2 = sb(st, "ucol2", [128, 22], F32)
                gcol = sb(st, "gcol", [128, 11], BF16)

                def fcw(cc_global, k):
                    return vcol(l, V_FCW + cc_global * 3 + k)

                def load(ti):
                    t0 = tiles[ti][0]
                    S.dma("sp", xt[ti % 2][:, :, :], XM_d[t0:t0 + TT, :].rearrange("(b p) d -> p b d", p=128),
                          writes=xt[ti % 2].r)

                nu = [0]
                nx = [0]
                wk = sb(st, "wk", [128, 3, 22], F32)
                for k in range(3):
                    for vg in range(2):
                        c0 = V_FCT + k * 44 + vg * 22 + v0
                        S.op("dve", CP(wk[:, k, vg * 11:(vg + 1) * 11], vecs[:, l, c0:c0 + 11]), reads=vecs.r,
                             writes=wk.r)
                cf = sb(st, "cf", [128, 22, 2], F32)
                cft = sb(st, "cft", [128, 22], F32)
                cft2 = sb(st, "cft2", [128, 22], F32)
                xl = [sb(st, f"xl3{i}", [128, D], F32, 2) for i in range(6)]

                def out_finish(xl_, np_, tok_lo, p_lo):
                    if final:
                        k = nx[0] % 2
                        S.op("dve", MSET(ss2[:, k:k + 1], 0.0), writes=ss2.r)
                        S.op("act", ACTF(junk[0:np_, 0, :], xl_[0:np_, :], AF.Square, accum_out=ss2[0:np_, k:k + 1]),
                             reads=xl_.r, writes=rr(junk, ss2))
                        S.op("act", ACTF(rs2[0:np_, k:k + 1], ss2[0:np_, k:k + 1], AF.Sqrt, bias=epsT[0:np_, 0:1],
                                         scale=1.0 / D), reads=rr(ss2, epsT), writes=rs2.r)
                        S.op("dve", RECIP(rs2[0:np_, k:k + 1], rs2[0:np_, k:k + 1]), reads=rs2.r, writes=rs2.r)
                        S.op("dve", STT(xl_[0:np_, :], xl_[0:np_, :], rs2[0:np_, k:k + 1], fg[0:np_, :],
                                        ALU.mult, ALU.mult), reads=rr(xl_, rs2, fg), writes=xl_.r)
                    S.dma("sp", dst_d[tok_lo + p_lo:tok_lo + np_, :], xl_[p_lo:np_, :], reads=xl_.r)

                def res_load(np_, tok_lo, p_lo):
                    xl_ = xl[nx[0] % 6]
                    nx[0] += 1
                    if p_lo:
                        S.op("dve", MSET(xl_[0:1, :], 0.0), writes=xl_.r)
                    S.dma("sp", xl_[p_lo:np_, :], res_d[tok_lo + p_lo:tok_lo + np_, :], writes=xl_.r)
                    return xl_

                def wdown_tile(ti):
                    t0, s0, sl, j, nt = tiles[ti]
                    xls = []
                    for b in range(4):
                        p_lo = 1 if (j == 0 and b == 0) else 0
                        xls.append(res_load(128, t0 - 1 + b * 128, p_lo))
                    for f in range(2):
                        accs = [nps() for b in range(4)]
                        for v in range(11):
                            S.op("pe", [MM(accs[b][:, :], G[:, v, b * 128:(b + 1) * 128], wDn[:, v, f * 512:(f + 1) * 512],
                                           v == 0, v == 10) for b in range(4)], reads=G.ri(v) + wDn.ri(v),
                                 writes=rr(*accs))
                        for b in range(4):
                            S.op("dve", TTO(xls[b][:, f * 512:(f + 1) * 512], accs[b][:, :],
                                            xls[b][:, f * 512:(f + 1) * 512], ALU.add), reads=accs[b].r + xls[b].ri(f),
                                 writes=xls[b].ri(f))
                    return xls

                def wdown_fin(ti, xls):
                    t0, s0, sl, j, nt = tiles[ti]
                    if final:
                        S.op("dve", MSET(ss4[:, :], 0.0), writes=ss4.r)
                        for b in range(4):
                            S.op("act", ACTF(junk[:, b, :], xls[b][:, :], AF.Square, accum_out=ss4[:, b:b + 1]),
                                 reads=xls[b].r, writes=junk.ri(b) + ss4.r)
                        S.op("act", ACTF(rs4[:, :], ss4[:, :], AF.Sqrt, bias=epsT[:, 0:1], scale=1.0 / D),
                             reads=rr(ss4, epsT), writes=rs4.r)
                        S.op("dve", RECIP(rs4[:, :], rs4[:, :]), reads=rs4.r, writes=rs4.r)
                        for b in range(4):
                            S.op("dve", STT(xls[b][:, :], xls[b][:, :], rs4[:, b:b + 1], fg[:, :], ALU.mult, ALU.mult),
                                 reads=rr(xls[b], rs4, fg), writes=xls[b].r)
                    for b in range(4):
                        p_lo = 1 if (j == 0 and b == 0) else 0
                        tok_lo = t0 - 1 + b * 128
                        S.dma("sp", dst_d[tok_lo + p_lo:tok_lo + 128, :], xls[b][p_lo:128, :], reads=xls[b].r)

                def tail(ti, yc):
                    t0, s0, sl, j, nt = tiles[ti]
                    S.op("dve", TTO(ucol[:, :], wk[:, 0, :], yc[:, :, 0], ALU.mult), reads=rr(wk, yc), writes=ucol.r)
                    S.op("dve", TTO(ucol2[:, :], wk[:, 1, :], yc[:, :, 1], ALU.mult), reads=rr(wk, yc), writes=ucol2.r)
                    S.op("dve", TTO(ucol2[:, :], ucol2[:, :], ucol[:, :], ALU.add), reads=rr(ucol, ucol2), writes=ucol2.r)
                    S.op("act", ACTF(ucol[:, 11:22], ucol2[:, 11:22], AF.Silu), reads=ucol2.r, writes=ucol.r)
                    S.op("dve", TTO(gcol[:, :], ucol2[:, 0:11], ucol[:, 11:22], ALU.mult), reads=rr(ucol, ucol2),
                         writes=gcol.r)
                    tok = s0 + sl - 1
                    xl_ = res_load(1, tok, 0)
                    for f in range(2):
                        pw = nps()
                        S.op("pe", [MM(pw[0:1, :], gcol[:, v:v + 1], wDn[:, v, f * 512:(f + 1) * 512], v == 0, v == 10)
                                    for v in range(11)], reads=rr(gcol, wDn), writes=pw.r)
                        S.op("dve", TTO(xl_[0:1, f * 512:(f + 1) * 512], pw[0:1, :], xl_[0:1, f * 512:(f + 1) * 512],
                                        ALU.add), reads=pw.r + xl_.ri(f), writes=xl_.ri(f))
                    out_finish(xl_, 1, tok, 0)

                def do_norm1(ti):
                    X = xt[ti % 2]
                    norm1(lambda b: X[:, b, :], lambda b: X.r, hn, ssq, rstd, junk)

                def yphase(ti):
                    t0, s0, sl, j, nt = tiles[ti]
                    hT = hTs[ti % 2]
                    yc_old = ycar[j % 2]
                    yc_new = ycar[(j + 1) % 2]
                    if j == 0:
                        S.op("dve", MSET(cf[:, :, :], 0.0), writes=cf.r)
                    else:
                        S.op("dve", TTO(cft[:, :], wk[:, 1, :], yc_old[:, :, 1], ALU.mult), reads=rr(wk, yc_old),
                             writes=cft.r)
                        S.op("dve", TTO(cft2[:, :], wk[:, 0, :], yc_old[:, :, 0], ALU.mult), reads=rr(wk, yc_old),
                             writes=cft2.r)
                        S.op("dve", TTO(cf[:, :, 0], cft[:, :], cft2[:, :], ALU.add), reads=rr(cft, cft2), writes=cf.r)
                        S.op("dve", TTO(cf[:, :, 1], wk[:, 0, :], yc_old[:, :, 1], ALU.mult), reads=rr(wk, yc_old),
                             writes=cf.r)
                    for v in range(11):
                        us = []
                        for vg in (1, 0):
                            ci = vg * 11 + v
                            py = nps()
                            S.op("pe", [MM(py[:, :], wUp[:, kc, vg, v * 128:(v + 1) * 128], hT[:, kc, :], kc == 0, kc == 7)
                                        for kc in range(8)], reads=wUp.ri(vg * 11 + v) + hT.r, writes=py.r)
                            u = ur[nu[0] % 6]
                            nu[0] += 1
                            w0, w1, w2 = (wk[:, k, ci:ci + 1] for k in range(3))
                            S.op("act", [lambda e, u=u, py=py, w2=w2: e.mul(out=u[:, 2:TT], in_=py[:, 2:TT], mul=w2),
                                         ACTF(u[:, 0:1], py[:, 0:1], AF.Identity, bias=cf[:, ci, 0:1], scale=w2),
                                         ACTF(u[:, 1:2], py[:, 1:2], AF.Identity, bias=cf[:, ci, 1:2], scale=w2)],
                                 reads=rr(py, wk, cf), writes=u.r)
                            S.op("act", ACTF(yc_new[:, ci, :], py[:, TT - 2:TT], AF.Identity), reads=py.r,
                                 writes=yc_new.ri(ci))
                            S.op("dve", STT(u[:, 1:TT], py[:, 0:TT - 1], w1, u[:, 1:TT], ALU.mult, ALU.add),
                                 reads=rr(py, wk, u), writes=u.r)
                            S.op("dve", STT(u[:, 2:TT], py[:, 0:TT - 2], w0, u[:, 2:TT], ALU.mult, ALU.add),
                                 reads=rr(py, wk, u), writes=u.r)
                            us.append(u)
                        ug, uv = us
                        sg_ = sgt[v % 4]
                        S.op("act", ACTF(sg_[:, :], ug[:, :], AF.Silu), reads=ug.r, writes=sg_.r)
                        S.op("dve", TTO(G[:, v, :], uv[:, :], sg_[:, :], ALU.mult), reads=rr(uv, sg_), writes=G.ri(v))

                def hstore(ti):
                    t0 = tiles[ti][0]
                    S.dma("sp", HT_d[:, t0:t0 + TT].rearrange("(c p) t -> p c t", p=128), hTs[ti % 2][:, :, :],
                          reads=hTs[ti % 2].r)

                def hload(ti):
                    t0 = tiles[ti][0]
                    S.dma("sp", hTs[ti % 2][:, :, :], HT_d[:, t0:t0 + TT].rearrange("(c p) t -> p c t", p=128),
                          writes=hTs[ti % 2].r)

                if half == 0:
                    load(0)
                    do_norm1(0)
                    norm2(hn, hTs[0], l, V_G2)
                    hstore(0)
                else:
                    hload(0)
                for ti, (t0, s0, sl, j, nt) in enumerate(tiles):
                    nxt = ti + 1 < len(tiles)
                    if nxt:
                        if half == 0:
                            load(ti + 1)
                        else:
                            hload(ti + 1)
                    yphase(ti)
                    if nxt and half == 0:
                        do_norm1(ti + 1)
                    xls = wdown_tile(ti)
                    if nxt and half == 0:
                        norm2(hn, hTs[(ti + 1) % 2], l, V_G2)
                        hstore(ti + 1)
                    wdown_fin(ti, xls)
                    if j == nt - 1:
                        tail(ti, ycar[(j + 1) % 2])
                S.barrier()

        cur = x_d
        stop = False
        for l in range(depth):
            last = (l == depth - 1)
            for name, fn in (("p1", lambda: phase1(l, cur)),
                             ("p2a", lambda: phase2a(l)),
                             ("p2b", lambda: phase2b(l, cur)),
                             ("p3a", lambda: phase3(l, 0, XM_d, PA_d, False)),
                             ("p3b", lambda: phase3(l, 1, PA_d, y_d if last else XL_d, last))):
                fn()
                if upto == f"{name}.{l}":
                    stop = True
                    break
            if stop:
                break
            cur = XL_d
        S.barrier()
        print("instructions:", S.nins, "ops:", S.nops, "engine counts:", S.cnt)
    return nc


def _host_consts(inp):
    f32 = np.float32
    vecs = np.zeros((L, 128, NV), f32)

    def pc(v, n):
        return np.ascontiguousarray(np.asarray(v, f32).reshape(n, 128).T)

    for l in range(L):
        vecs[l, :, V_G1:V_G1 + 8] = pc(inp["norm1_g"][l], 8)
        vecs[l, :, V_G2:V_G2 + 8] = pc(inp["norm2_g"][l], 8)
        for i in range(4):
            vecs[l, :, V_BG + i * 8:V_BG + (i + 1) * 8] = pc(inp["b_gate"][l, i], 8)
        vecs[l, :, V_CAB:V_CAB + 2] = pc(inp["conv_a_b"][l], 2)
        vecs[l, :, V_LNG:V_LNG + 2] = pc(inp["ln_a_g"][l], 2)
        vecs[l, :, V_LNB:V_LNB + 2] = pc(inp["ln_a_b"][l], 2)
        vecs[l, :, V_PSC:V_PSC + 2] = pc(inp["pool_scale"][l], 2)
        for c in range(2):
            for k in range(31):
                vecs[l, :, V_CAW + c * 31 + k] = inp["conv_a_w"][l, k, c * 128:(c + 1) * 128]
            for k in range(3):
                vecs[l, :, V_SCW + c * 3 + k] = inp["sc_w"][l, k, c * 128:(c + 1) * 128]
        for cc in range(44):
            for k in range(3):
                vecs[l, :, V_FCW + cc * 3 + k] = inp["ffn_conv_w"][l, k, cc * 128:(cc + 1) * 128]
                vecs[l, :, V_FCT + k * 44 + cc] = inp["ffn_conv_w"][l, k, cc * 128:(cc + 1) * 128]
    for l in range(L):
        caw = np.asarray(inp["conv_a_w"][l], f32)
        for g in range(8):
            for q in range(8):
                for s_ in range(4):
                    k = 4 * q + s_
                    if k < 31:
                        vecs[l, 32 * s_:32 * s_ + 32, V_WST + g * 8 + q] = caw[k, g * 32:(g + 1) * 32]
    pbd = np.zeros((L, 2, 128, 128), f32)
    for l in range(L):
        for c in range(2):
            pbd[l, c, 0:64, 0:64] = inp["pool_w"][l, 2 * c]
            pbd[l, c, 64:128, 64:128] = inp["pool_w"][l, 2 * c + 1]
    kc = np.arange(64)[:, None]
    qc = np.arange(64)[None, :]
    idx = np.clip(kc - qc + 15, 0, 30)
    rpb = np.asarray(inp["rpb"], f32)
    tbh = np.ascontiguousarray(rpb[:, :, :, idx].transpose(0, 3, 1, 2, 4))
    cstart = np.clip(qc - 8, 0, 48)
    cm = np.where((kc >= cstart) & (kc < cstart + 16), 0.0, -1e30).astype(f32)
    cmf = np.ascontiguousarray(np.broadcast_to(np.concatenate([cm, cm], 0)[:, None, :], (128, 14, 64))).astype(f32)
    icnt = np.zeros((4, 128, 2, TT), f32)
    pos = np.arange(TT)
    for var in range(4):
        first, last = var & 1, (var >> 1) & 1
        for g, w in enumerate(POOL_WINDOWS):
            lo = pos - w // 2
            hi = pos + w // 2 - 1
            if first:
                lo = np.maximum(lo, 0)
            if last:
                hi = np.minimum(hi, TT - 1)
            cnt = (hi - lo + 1).astype(f32)
            c, half = g // 2, g % 2
            icnt[var, half * 64:(half + 1) * 64, c, :] = (1.0 / cnt)[None, :]
    fgb = np.ascontiguousarray(np.broadcast_to(np.asarray(inp["final_g"], f32)[None, :], (128, D)))
    return dict(vecs=vecs, pbd=pbd, tbh=tbh, cmf=cmf, icnt=icnt, fgb=fgb)


def _shared_inputs(inp):
    d = _host_consts(inp)
    for k in ("w_in", "w_o", "w_up", "w_down"):
        d[k] = np.ascontiguousarray(np.asarray(inp[k], np.float32))
    d["w_out_a"] = np.ascontiguousarray(np.asarray(inp["w_out_a"], np.float32))
    d["w_out_b"] = np.ascontiguousarray(np.asarray(inp["w_out_b"], np.float32))
    d["w_out_c"] = np.ascontiguousarray(np.asarray(inp["w_out_c"], np.float32))
    d["w_out_d"] = np.ascontiguousarray(np.asarray(inp["w_out_d"], np.float32))
    return d


def kernel(**inputs):
    xp = np.asarray(inputs["x_prompt"], np.float32)
    xs = np.asarray(inputs["x_sample"], np.float32)
    n = 8
    shared = _shared_inputs(inputs)
    in_maps = []
    for c in range(n):
        xc = np.concatenate([xp[2 * c].reshape(-1, D), xp[2 * c + 1].reshape(-1, D), xs[c].reshape(-1, D)], axis=0)
        m = dict(shared)
        m["x"] = np.ascontiguousarray(xc)
        in_maps.append(m)
    nc = build_nc([4096, 4096, 8192])
    res = run_bass_kernel_spmd(nc, in_maps, core_ids=list(range(n)))
    yp = np.empty_like(xp)
    ys = np.empty_like(xs)
    for c in range(n):
        y = res.results[c]["y"]
        yp[2 * c] = y[0:4096]
        yp[2 * c + 1] = y[4096:8192]
        ys[c] = y[8192:16384]
    return (yp, ys)
```
